# Optimizing a Trainium2 kernel written in Bass

```python
import jax, jax.numpy as jnp
from jax import lax
import numpy as np

D_MODEL = 1024
BATCH = 2
SEQ = 16384
DEPTH = 2

N_A = DEPTH // 2
N_B = DEPTH - N_A
CHUNK = 64
LEFT_CHUNKS = 8
BAND = (LEFT_CHUNKS + 1) * CHUNK
PAD = LEFT_CHUNKS * CHUNK
D_RNN = D_MODEL
LRU_BLOCKS = 8
LRU_BW = D_RNN // LRU_BLOCKS
CONV_W = 4
LRU_C = 8.0
N_HEADS = 16
HEAD_DIM = D_MODEL // N_HEADS
MAX_REL = 2 * CHUNK
MIN_REL = -(CHUNK - 1)
NREL = MAX_REL - MIN_REL + 1
D_FF = 4 * D_MODEL
EPS = 1e-6

kernel_name = 'yoco_rglru_chunk_relbias_hybrid'


def rmsnorm(x, g):
    xf = x.astype(jnp.float32)
    y = xf * lax.rsqrt(jnp.mean(xf * xf, axis=-1, keepdims=True) + EPS)
    return (y * g.astype(jnp.float32)).astype(x.dtype)


def causal_depthwise_conv(x, w, b):
    s = x.shape[1]
    xp = jnp.pad(x, ((0, 0), (CONV_W - 1, 0), (0, 0)))
    out = b + xp[:, 0:s] * w[0]
    for k in range(1, CONV_W):
        out = out + xp[:, k:k + s] * w[k]
    return out


def rg_lru(x, w_gate, b_gate, lam):
    bsz, s, _ = x.shape
    xf = x.astype(jnp.float32)
    xb = xf.reshape(bsz, s, LRU_BLOCKS, LRU_BW)
    g = jnp.einsum('bsnd,nde->bsne', xb, w_gate.astype(jnp.float32)) + b_gate.astype(jnp.float32)
    r = jax.nn.sigmoid(g[..., :LRU_BW]).reshape(bsz, s, D_RNN)
    i = jax.nn.sigmoid(g[..., LRU_BW:]).reshape(bsz, s, D_RNN)
    log_a = -LRU_C * r * jax.nn.softplus(-lam.astype(jnp.float32))
    a = jnp.exp(log_a)
    b = jnp.sqrt(-jnp.expm1(2.0 * log_a)) * (i * xf)

    def combine(left, right):
        a1, b1 = left
        a2, b2 = right
        return a1 * a2, a2 * b1 + b2

    _, h = lax.associative_scan(combine, (a, b), axis=1)
    return h.astype(x.dtype)


def recurrent_block(x, w_in, conv_w, conv_b, w_gate, b_gate, lam, w_out):
    u = x @ w_in
    gate, rec = u[..., :D_RNN], u[..., D_RNN:]
    rec = causal_depthwise_conv(rec, conv_w, conv_b)
    h = rg_lru(rec, w_gate, b_gate, lam)
    return (jax.nn.gelu(gate) * h) @ w_out


def shared_kv(x, kv_norm, w_kv, k_norm):
    bsz, s, _ = x.shape
    h = rmsnorm(x, kv_norm)
    kv = (h @ w_kv).reshape(bsz, s, 2, N_HEADS, HEAD_DIM)
    k = rmsnorm(kv[:, :, 0], k_norm)
    v = kv[:, :, 1]
    k = jnp.pad(k.transpose(0, 2, 1, 3), ((0, 0), (0, 0), (PAD, 0), (0, 0)))
    v = jnp.pad(v.transpose(0, 2, 1, 3), ((0, 0), (0, 0), (PAD, 0), (0, 0)))
    return k, v


def chunk_band_attention(q, k_pad, v_pad, rel_bias):
    bsz, s = q.shape[:2]
    nc = s // CHUNK
    qc = q.reshape(bsz, nc, CHUNK, N_HEADS, HEAD_DIM).transpose(1, 0, 3, 2, 4)
    qi = jnp.arange(CHUNK)[:, None]
    kj = jnp.arange(BAND)[None, :]
    dist = qi + PAD - kj
    idx = jnp.clip(dist, MIN_REL, MAX_REL) - MIN_REL
    bias = rel_bias.astype(jnp.float32)[:, idx]
    scale = HEAD_DIM ** -0.5

    def one_chunk(args):
        c, qb = args
        kb = lax.dynamic_slice_in_dim(k_pad, c * CHUNK, BAND, axis=2)
        vb = lax.dynamic_slice_in_dim(v_pad, c * CHUNK, BAND, axis=2)
        sc = jnp.einsum('bhqd,bhkd->bhqk', qb, kb).astype(jnp.float32) * scale + bias
        valid = (c * CHUNK - PAD + jnp.arange(BAND)) >= 0
        sc = jnp.where(valid, sc, -jnp.inf)
        p = jax.nn.softmax(sc, axis=-1).astype(vb.dtype)
        return jnp.einsum('bhqk,bhkd->bhqd', p, vb)

    o = lax.map(one_chunk, (jnp.arange(nc), qc))
    return o.transpose(1, 0, 3, 2, 4).reshape(bsz, s, N_HEADS * HEAD_DIM)


def sqrelu_mlp(x, w_up, w_down):
    return jnp.square(jax.nn.relu(x @ w_up)) @ w_down


def setup_inputs(seed: int = 0) -> dict:
    key = jax.random.key(seed)
    ks = jax.random.split(key, 24)
    f32 = jnp.float32

    def nrm(k, shape, fan_in):
        return jax.random.normal(k, shape, f32) * (fan_in ** -0.5)

    def gain(k, shape):
        return 1.0 + 0.05 * jax.random.normal(k, shape, f32)

    x = jax.random.normal(ks[0], (BATCH, SEQ, D_MODEL), f32)
    a_norm = gain(ks[1], (N_A, D_MODEL))
    a_w_in = nrm(ks[2], (N_A, D_MODEL, 2 * D_RNN), D_MODEL)
    a_conv_w = nrm(ks[3], (N_A, CONV_W, D_RNN), CONV_W)
    a_conv_b = 0.01 * jax.random.normal(ks[4], (N_A, D_RNN), f32)
    a_w_gate = nrm(ks[5], (N_A, LRU_BLOCKS, LRU_BW, 2 * LRU_BW), LRU_BW)
    a_b_gate = 0.01 * jax.random.normal(ks[6], (N_A, LRU_BLOCKS, 2 * LRU_BW), f32)
    u = jax.random.uniform(ks[7], (N_A, D_RNN), f32, 0.9, 0.999)
    base = u ** (1.0 / LRU_C)
    a_lambda = jnp.log(base) - jnp.log1p(-base)
    a_w_out = nrm(ks[8], (N_A, D_RNN, D_MODEL), D_RNN)
    kv_norm = gain(ks[9], (D_MODEL,))
    w_kv = nrm(ks[10], (D_MODEL, 2 * N_HEADS * HEAD_DIM), D_MODEL)
    k_norm = gain(ks[11], (HEAD_DIM,))
    b_norm = gain(ks[12], (N_B, D_MODEL))
    b_w_q = nrm(ks[13], (N_B, D_MODEL, N_HEADS * HEAD_DIM), D_MODEL)
    b_q_norm = gain(ks[14], (N_B, HEAD_DIM))
    b_rel_bias = 0.1 * jax.random.normal(ks[15], (N_B, N_HEADS, NREL), f32)
    b_w_o = nrm(ks[16], (N_B, N_HEADS * HEAD_DIM, D_MODEL), N_HEADS * HEAD_DIM)
    mlp_norm = gain(ks[17], (DEPTH, D_MODEL))
    w_up = nrm(ks[18], (DEPTH, D_MODEL, D_FF), D_MODEL)
    w_down = nrm(ks[19], (DEPTH, D_FF, D_MODEL), D_FF)
    return {'x': x, 'a_norm': a_norm, 'a_w_in': a_w_in, 'a_conv_w': a_conv_w,
            'a_conv_b': a_conv_b, 'a_w_gate': a_w_gate, 'a_b_gate': a_b_gate,
            'a_lambda': a_lambda, 'a_w_out': a_w_out, 'kv_norm': kv_norm, 'w_kv': w_kv,
            'k_norm': k_norm, 'b_norm': b_norm, 'b_w_q': b_w_q, 'b_q_norm': b_q_norm,
            'b_rel_bias': b_rel_bias, 'b_w_o': b_w_o, 'mlp_norm': mlp_norm,
            'w_up': w_up, 'w_down': w_down}


def reference(x, a_norm, a_w_in, a_conv_w, a_conv_b, a_w_gate, a_b_gate, a_lambda,
              a_w_out, kv_norm, w_kv, k_norm, b_norm, b_w_q, b_q_norm, b_rel_bias,
              b_w_o, mlp_norm, w_up, w_down):
    bsz, s, _ = x.shape
    h = x
    k_pad = None
    v_pad = None
    for l in range(DEPTH):
        if l < N_A:
            h = h + recurrent_block(rmsnorm(h, a_norm[l]), a_w_in[l], a_conv_w[l], a_conv_b[l],
                                    a_w_gate[l], a_b_gate[l], a_lambda[l], a_w_out[l])
        else:
            if l == N_A:
                k_pad, v_pad = shared_kv(h, kv_norm, w_kv, k_norm)
            j = l - N_A
            q = (rmsnorm(h, b_norm[j]) @ b_w_q[j]).reshape(bsz, s, N_HEADS, HEAD_DIM)
            q = rmsnorm(q, b_q_norm[j])
            o = chunk_band_attention(q, k_pad, v_pad, b_rel_bias[j])
            h = h + o @ b_w_o[j]
        h = h + sqrelu_mlp(rmsnorm(h, mlp_norm[l]), w_up[l], w_down[l])
    return h
```

```python
import contextlib
import numpy as np
import concourse.bass as bass
import concourse.mybir as mybir
from concourse.bass_utils import run_bass_kernel_spmd

F32 = mybir.dt.float32
BF16 = mybir.dt.bfloat16
U8 = mybir.dt.uint8
AF = mybir.ActivationFunctionType
ALU = mybir.AluOpType
AX = mybir.AxisListType

NCORES = 8
D = 1024
T = 512
NT = 8
SEG = T * NT
NPRE = 23
TOK = NPRE * T + T + SEG
NH = 16
EPS = 1e-6
NSLOT = 46


class Prog:
    ENGS = ('pe', 'act', 'dve', 'pool', 'sp')

    def __init__(self):
        self.ops = {e: [] for e in self.ENGS}
        self.res = {}
        self.seen = {e: {} for e in self.ENGS}
        self.dma_count = {}
        self.dma_inc = {}

    def op(self, eng, fn, R=(), W=(), dma=None, inc=16):
        deps = []
        for k in R:
            st = self.res.get(k)
            if st is not None and st[0] is not None:
                deps.append(st[0])
        for k in W:
            st = self.res.get(k)
            if st is not None:
                if st[0] is not None:
                    deps.append(st[0])
                deps.extend(st[1].items())
        idx = len(self.ops[eng])
        waits = {}
        seen = self.seen[eng]
        for (sk, v) in deps:
            if sk == 'pe' and eng == 'pe':
                continue
            if seen.get(sk, -1) >= v:
                continue
            if waits.get(sk, -1) < v:
                waits[sk] = v
        for sk, v in waits.items():
            seen[sk] = v
        if dma is None:
            ev = (eng, idx)
        else:
            dma = dma + '_' + eng
            c = self.dma_count.get(dma, 0) + 1
            self.dma_count[dma] = c
            self.dma_inc[dma] = inc
            ev = ('dma:' + dma, c)
        self.ops[eng].append(dict(fn=fn, waits=waits, flag=False, dma=dma))
        for k in R:
            st = self.res.setdefault(k, [None, {}])
            if st[1].get(ev[0], -1) < ev[1]:
                st[1][ev[0]] = ev[1]
        for k in W:
            self.res[k] = [ev, {}]
        return ev

    def emit(self, nc):
        ops = self.ops
        for e in self.ENGS:
            for o in ops[e]:
                for sk, v in o['waits'].items():
                    if not sk.startswith('dma:'):
                        ops[sk][v]['flag'] = True
        for e in self.ENGS:
            c = 0
            for o in ops[e]:
                if o['flag']:
                    c += 1
                o['rank'] = c
        with contextlib.ExitStack() as st:
            sems = {}
            for e in self.ENGS:
                sems[e] = st.enter_context(nc.semaphore('s_' + e))
            for k in self.dma_count:
                sems['dma:' + k] = st.enter_context(nc.semaphore('d_' + k))
            block = st.enter_context(nc.Block())

            def replay(ename):
                def body(eng):
                    for o in ops[ename]:
                        for sk, v in o['waits'].items():
                            if sk.startswith('dma:'):
                                eng.wait_ge(sems[sk], self.dma_inc[sk[4:]] * v)
                            else:
                                eng.wait_ge(sems[sk], ops[sk][v]['rank'])
                        if o['fn'] is None:
                            continue
                        ins = o['fn'](eng)
                        if o['dma'] is not None:
                            ins.then_inc(sems['dma:' + o['dma']], self.dma_inc[o['dma']])
                        elif o['flag']:
                            ins.then_inc(sems[ename], 1)
                return body

            block.tensor(replay('pe'))
            block.scalar(replay('act'))
            block.vector(replay('dve'))
            block.gpsimd(replay('pool'))
            block.sync(replay('sp'))
        return nc


C_ANORM, C_CONVW, C_CONVB, C_BGR, C_BGI, C_LAM = 0, 8, 40, 48, 56, 64
C_MLP0, C_MLP1, C_KVN, C_BN = 72, 80, 88, 96
C_GK, C_GQ, C_HASPREV = 104, 105, 106
C_VALID = 107
NCST = C_VALID + NPRE


def build_nc():
    nc = bass.Bass("TRN2", target_bir_lowering=False)
    dten = nc.dram_tensor
    xT = dten("xT", [D, TOK], F32, kind="ExternalInput").ap()
    w_in = dten("w_in", [D, 2 * D], F32, kind="ExternalInput").ap()
    w_gate = dten("w_gate", [8, 128, 256], F32, kind="ExternalInput").ap()
    w_out = dten("w_out", [D, D], F32, kind="ExternalInput").ap()
    w_kv = dten("w_kv", [D, 2 * D], F32, kind="ExternalInput").ap()
    w_q = dten("w_q", [D, D], F32, kind="ExternalInput").ap()
    w_o = dten("w_o", [D, D], F32, kind="ExternalInput").ap()
    w_up = dten("w_up", [2, D, 4 * D], F32, kind="ExternalInput").ap()
    w_down = dten("w_down", [2, 4 * D, D], F32, kind="ExternalInput").ap()
    cst_d = dten("cst", [128, NCST], F32, kind="ExternalInput").ap()
    relb_d = dten("relb", [128, NH, 640], F32, kind="ExternalInput").ap()
    mats_d = dten("mats", [128, 3, 128], F32, kind="ExternalInput").ap()
    outT = dten("outT", [D, SEG], F32, kind="ExternalOutput").ap()
    wscr = dten("wscr", [NSLOT, 128, 4096], BF16, kind="Internal").ap()

    xv = xT.rearrange("(c p) t -> p c t", p=128)
    ov = outT.rearrange("(c p) t -> p c t", p=128)

    def wsrc(w2d, r0, c0):
        return w2d[r0:r0 + 1024, c0:c0 + 512].rearrange("(kc p) j -> p kc j", p=128)
    slot_src = []
    for i in range(4):
        slot_src.append(wsrc(w_in, 0, i * 512))
    for i in range(2):
        slot_src.append(wsrc(w_out, 0, i * 512))
    for i in range(8):
        slot_src.append(wsrc(w_up[0], 0, i * 512))
    for ch in range(2):
        for kg in range(4):
            slot_src.append(wsrc(w_down[0], kg * 1024, ch * 512))
    for i in range(4):
        slot_src.append(wsrc(w_kv, 0, i * 512))
    for i in range(2):
        slot_src.append(wsrc(w_q, 0, i * 512))
    for i in range(2):
        slot_src.append(wsrc(w_o, 0, i * 512))
    for i in range(8):
        slot_src.append(wsrc(w_up[1], 0, i * 512))
    for ch in range(2):
        for kg in range(4):
            slot_src.append(wsrc(w_down[1], kg * 1024, ch * 512))
    assert len(slot_src) == NSLOT
    S_WIN, S_WOUT, S_UP0, S_DN0, S_KV, S_Q, S_O, S_UP1, S_DN1 = 0, 4, 6, 14, 22, 26, 28, 30, 38

    P = Prog()
    with contextlib.ExitStack() as st:
        sb = lambda name, shape, dt: st.enter_context(nc.sbuf_tensor(name, shape, dt))
        CST = sb("cstt", [128, NCST], F32)
        DER = sb("der", [128, 64], F32)
        MATS = sb("matsb", [128, 3, 128], BF16)
        WG = sb("wg", [128, 8, 256], F32)
        EB = sb("eb", [128, 8, 5, 2, 128], BF16)
        WR = sb("wr", [128, 4, 4096], BF16)
        RH1 = sb("rh1", [128, 8, 4], F32)
        CAR1 = sb("car1", [128, 8], F32)
        TSJ = sb("tsj", [128, 8], F32)
        ARENA_BYTES = 108 * 1024 + 16384 + 16896
        AR = sb("arena", [128, ARENA_BYTES], U8)
        PS = st.enter_context(nc.psum_tensor("ps", [128, 8, 512], F32))

        O_KR = 108 * 1024
        O_VR = O_KR + 16384
        KR = AR[:, O_KR:O_KR + 16384].bitcast(BF16).rearrange("p (a m t) -> p a m t", a=2, m=8)
        VR = AR[:, O_VR:O_VR + 16896].bitcast(BF16).rearrange("p (a k h d) -> p a k h d", a=2, k=4, h=NH)
        ONES = MATS[:, 0, :]
        BONES = MATS[:, 1, :]
        IDENT = MATS[:, 2, :]

        def av(off, n, dt):
            nb = n * (4 if dt == F32 else 2)
            return AR[:, off:off + nb].bitcast(dt)

        def pk(off, nbytes):
            return [('ar', i) for i in range(off // 1024, (off + nbytes + 1023) // 1024)]

        O_X = [0, 16384]
        O_XN = [32768, 40960]
        O_SQ = 49152
        O_RS = [57344, 59392]
        O_U = 61440

        def Xc(b, c, n=T):
            return av(O_X[b] + c * 2048, T, F32)[:, 0:n]

        def Xk(b, c):
            return pk(O_X[b] + c * 2048, 2048)

        def XNc(b, c, n=T):
            return av(O_XN[b] + c * 1024, T, BF16)[:, 0:n]

        def XNk(b, c):
            return pk(O_XN[b] + c * 1024, 1024)

        def SQc(c, n=T):
            return av(O_SQ + c * 1024, T, BF16)[:, 0:n]

        def SQk(c):
            return pk(O_SQ + c * 1024, 1024)

        def RSb(j, n=T):
            return av(O_RS[j], T, F32)[:, 0:n]

        def RSk(j):
            return pk(O_RS[j], 2048)

        LT_MODE = ['main']

        def LT(b, name):
            if LT_MODE[0] == 'p1':
                base = (O_U + b * 10240) if b < 4 else (O_U + 40960 + (b - 4) * 10240)
                offs = dict(rec=0, s=0, cv=3072, hs=3072, r=5120, i=7168, cvb=9216)
            else:
                base = O_U + b * 16384
                offs = dict(rec=0, cv=3072, r=5120, i=7168, s=9216, hs=11264, u=13312, cvb=15360)
            o = base + offs[name]
            if name == 'rec':
                return av(o, 516, F32), pk(o, 3072)
            if name == 'cvb':
                return av(o, T, BF16), pk(o, 1024)
            if name == 's' and LT_MODE[0] == 'p1':
                return av(o, T, F32), pk(o, 3072)
            return av(o, T, F32), pk(o, 2048)
        O_HG = O_U + 40960

        def HGc(c):
            return av(O_HG + c * 1024, T, BF16)

        def HGk(c):
            return pk(O_HG + c * 1024, 1024)

        def HIDc(f):
            return av(O_U + f * 1024, T, BF16)

        def HIDk(f):
            return pk(O_U + f * 1024, 1024)

        def RTb(j):
            o = O_U + 32768 + j * 2048
            return av(o, T, F32), pk(o, 2048)

        O_QBD = O_U
        O_OT = O_U + 16384
        O_PE = O_U + 24576
        O_PM = O_PE + 2 * 2560
        O_OS = O_PM + 2 * 2560
        O_KT = O_OS + 4096
        O_RD = O_KT + 4096

        QBD = av(O_QBD, 8192, BF16).rearrange("p (m a h q) -> p m a h q", m=8, a=4, h=2)

        def QBDk(m):
            return pk(O_QBD + m * 2048, 2048)

        def OTc(m):
            return av(O_OT + m * 1024, T, BF16)

        def OTk(m):
            return pk(O_OT + m * 1024, 1024)

        def PEb(j):
            return av(O_PE + j * 2560, 1280, BF16), [('pexp', j)]

        def PMb(j):
            return av(O_PM + j * 2560, 1280, BF16), [('pm', j)]

        def OSb(j):
            return av(O_OS + j * 2048, 1024, BF16), [('osb', j)]

        def KTb(j):
            return av(O_KT + j * 2048, T, F32), [('ktb', j)]

        def RDb(j):
            return av(O_RD + j * 64, 16, F32), [('rdb', j)]
        ATT_PAGES = pk(O_PE, O_RD + 128 - O_PE)

        cc = lambda col, n=1: CST[:, col:col + n]
        dc = lambda col, n=1: DER[:, col:col + n]
        D_EPS, D_Q, D_ONE = 0, 1, 2
        D_HBGR, D_HBGI, D_C, D_HC, D_T1 = 8, 16, 24, 32, 56

        P.op('sp', lambda e: e.dma_start(out=CST[:], in_=cst_d), W=['cst'], dma='ld0')
        P.op('pool', lambda e: e.dma_start(out=MATS[:], in_=mats_d), W=['mats'], dma='ld1')
        P.op('sp', lambda e: e.dma_start(out=WG[:], in_=w_gate.rearrange("n d e -> d n e")), W=['wg'], dma='ld2')
        P.op('dve', lambda e: e.memset(DER[:, 0:1], EPS), W=['der'])
        P.op('dve', lambda e: e.memset(DER[:, 1:2], 0.25), R=['der'], W=['der'])
        P.op('dve', lambda e: e.memset(DER[:, 2:3], 1.0), R=['der'], W=['der'])
        P.op('dve', lambda e: e.memset(CAR1[:], 0.0), W=[('car1', m) for m in range(8)])
        P.op('dve', lambda e: e.memset(TSJ[:], 0.0), W=['tsj'])
        P.op('dve', lambda e: e.tensor_scalar(out=dc(D_HBGR, 16), in0=cc(C_BGR, 16), scalar1=0.5, scalar2=None, op0=ALU.mult),
             R=['cst', 'der'], W=['der'])
        P.op('act', lambda e: e.activation(out=dc(D_T1, 8), in_=cc(C_LAM, 8), func=AF.Exp, scale=-1.0), R=['cst', 'der'], W=['der'])
        P.op('act', lambda e: e.activation(out=dc(D_T1, 8), in_=dc(D_T1, 8), func=AF.Ln, bias=dc(D_ONE), scale=1.0), R=['der'], W=['der'])
        P.op('dve', lambda e: e.tensor_scalar(out=dc(D_C, 8), in0=dc(D_T1, 8), scalar1=-8.0, scalar2=None, op0=ALU.mult), R=['der'], W=['der'])
        P.op('dve', lambda e: e.tensor_scalar(out=dc(D_HC, 8), in0=dc(D_T1, 8), scalar1=-4.0, scalar2=None, op0=ALU.mult), R=['der'], W=['der'])
        for h in range(NH):
            stg = av(O_U + (h % 2) * 4096, 640, F32)
            sk = pk(O_U + (h % 2) * 4096, 2560)
            P.op('sp', lambda e, h=h, stg=stg: e.dma_start(out=stg, in_=relb_d[:, h, :]), W=sk, dma='ldb%d' % (h % 2))
            P.op('act', lambda e, h=h, stg=stg: e.activation(out=EB[:, h // 2, :, h % 2, :], in_=stg.rearrange("p (w q) -> p w q", w=5), func=AF.Exp), R=sk, W=[('eb', h)])
        def vones(hf):
            return VR[:, hf].rearrange("p k h d -> p (k h) d")[:, :, 64:66]

        ring_pos = [0]

        def wr_keys(rb):
            return [('wr', rb)]

        def load_slot(slot):
            rb = ring_pos[0] % 4
            ring_pos[0] += 1
            P.op('sp', lambda e, rb=rb, slot=slot: e.dma_start(out=WR[:, rb, :], in_=wscr[slot]),
                 R=[('scr', slot)], W=wr_keys(rb), dma='w%d' % rb)
            return rb

        def WS(rb, kc, c0, n=128):
            return WR[:, rb, kc * 512 + c0: kc * 512 + c0 + n]

        for i in (2, 3):
            P.op('pool', lambda e, i=i: e.dma_start(out=WR[:, i, :].rearrange("p (kc j) -> p kc j", kc=8), in_=slot_src[i]),
                 W=wr_keys(i), dma='w%d' % i)
        pc_cnt = [0]

        def precast(s):
            stg = pc_cnt[0] % 2
            pc_cnt[0] += 1
            P.op('pool', lambda e, s=s, stg=stg: e.dma_start(out=WR[:, stg, :].rearrange("p (kc j) -> p kc j", kc=8), in_=slot_src[s]),
                 W=wr_keys(stg), dma='w%d' % stg)
            P.op('sp', lambda e, s=s, stg=stg: e.dma_start(out=wscr[s], in_=WR[:, stg, :]), R=wr_keys(stg), W=[('scr', s)], dma='pc%d' % (s % 4))

        def load_x(b, t0, n=T, q='pool'):
            keys = [k for c in range(8) for k in Xk(b, c)]
            dst = av(O_X[b], 8 * T, F32).rearrange("p (c t) -> p c t", c=8)[:, :, 0:n]
            P.op(q, lambda e: e.dma_start(out=dst, in_=xv[:, :, t0:t0 + n]), W=keys, dma='x%s%d' % (q, b))

        def norm_stats(b, j, n=T, scale=1.0 / D):
            for c in range(8):
                P.op('act', lambda e, c=c: e.activation(out=SQc(c, n), in_=Xc(b, c, n), func=AF.Square), R=Xk(b, c), W=SQk(c))

            def mm(e):
                for c in range(8):
                    ins = e.matmul(PS[:, 7, 0:n], lhsT=ONES, rhs=SQc(c, n), start=(c == 0), stop=(c == 7))
                return ins
            P.op('pe', mm, R=[k for c in range(8) for k in SQk(c)] + ['mats'], W=[('ps', 7)])
            P.op('act', lambda e: e.activation(out=RSb(j, n), in_=PS[:, 7, 0:n], func=AF.Ln, scale=scale, bias=dc(D_EPS)),
                 R=[('ps', 7), 'der'], W=RSk(j))
            P.op('act', lambda e: e.activation(out=RSb(j, n), in_=RSb(j, n), func=AF.Exp, scale=-0.5), R=RSk(j), W=RSk(j))

        def norm_apply(b, j, xb, gcol, n=T):
            for c in range(8):
                P.op('dve', lambda e, c=c: e.scalar_tensor_tensor(out=XNc(xb, c, n), in0=Xc(b, c, n), scalar=cc(gcol + c), in1=RSb(j, n),
                                                                  op0=ALU.mult, op1=ALU.mult),
                     R=Xk(b, c) + RSk(j) + ['cst'], W=XNk(xb, c))

        XB = [0]

        def rec_mm(m, rbs, n=T, bank=None):
            bk = (m % 2) if bank is None else bank
            rb = rbs[m // 4]
            xb = XB[0]

            def mm(e):
                for kc in range(8):
                    ins = e.matmul(PS[:, bk, 0:n], lhsT=WS(rb, kc, (m % 4) * 128), rhs=XNc(xb, kc, n), start=(kc == 0), stop=(kc == 7))
                return ins
            P.op('pe', mm, R=[k for c in range(8) for k in XNk(xb, c)] + wr_keys(rb), W=[('ps', bk)])
            return bk

        NSETS = [2]

        def lru_A(m, RH, rhname):
            b = m % NSETS[0]
            rec, rec_k = LT(b, 'rec')
            cv, cv_k = LT(b, 'cv')
            cvb, cvb_k = LT(b, 'cvb')
            bk = m % 2
            P.op('act', lambda e: e.activation(out=rec[:, 0:3], in_=RH[:, m, 0:3], func=AF.Copy), R=[(rhname, m)], W=rec_k)
            yield
            P.op('act', lambda e: e.activation(out=rec[:, 3:3 + T], in_=PS[:, bk, :], func=AF.Copy), R=[('ps', bk)] + rec_k, W=rec_k)
            yield
            P.op('act', lambda e: e.activation(out=RH[:, m, 0:3], in_=rec[:, T:T + 3], func=AF.Copy), R=rec_k, W=[(rhname, m)])
            yield
            P.op('dve', lambda e: e.tensor_scalar(out=cv, in0=rec[:, 0:T], scalar1=cc(C_CONVW + m), scalar2=cc(C_CONVB + m), op0=ALU.mult, op1=ALU.add),
                 R=rec_k + ['cst'], W=cv_k)
            yield
            for k in range(1, 4):
                P.op('dve', lambda e, k=k: e.scalar_tensor_tensor(out=cv, in0=rec[:, k:k + T], scalar=cc(C_CONVW + 8 * k + m), in1=cv,
                                                                  op0=ALU.mult, op1=ALU.add),
                     R=rec_k + cv_k + ['cst'], W=cv_k)
                yield

        def lru_gates_mm(m):
            b = m % 2
            cv, cv_k = LT(m % NSETS[0], 'cv')

            def mm(e):
                e.matmul(PS[:, 4 + b, :], lhsT=WG[:, m, 0:128], rhs=cv, start=True, stop=True)
                return e.matmul(PS[:, 6 + b, :], lhsT=WG[:, m, 128:256], rhs=cv, start=True, stop=True)
            P.op('pe', mm, R=cv_k + ['wg'], W=[('ps', 4 + b), ('ps', 6 + b)])
            yield

        def lru_act_exp(m):
            b = m % 2
            r_, r_k = LT(m % NSETS[0], 'r')
            i_, i_k = LT(m % NSETS[0], 'i')
            s_, s_k = LT(m % NSETS[0], 's')
            P.op('act', lambda e: e.activation(out=r_, in_=PS[:, 4 + b, :], func=AF.Tanh, scale=0.5, bias=dc(D_HBGR + m)),
                 R=[('ps', 4 + b), 'der'], W=r_k)
            yield
            P.op('act', lambda e: e.activation(out=i_, in_=PS[:, 6 + b, :], func=AF.Tanh, scale=0.5, bias=dc(D_HBGI + m)),
                 R=[('ps', 6 + b), 'der'], W=i_k)
            yield
            P.op('act', lambda e: e.activation(out=r_, in_=r_, func=AF.Exp, scale=dc(D_HC + m), bias=dc(D_HC + m)), R=r_k + ['der'], W=r_k)
            yield
            P.op('pool', lambda e: e.tensor_tensor(out=s_, in0=r_, in1=r_, op=ALU.mult), R=r_k, W=s_k)
            yield

        def lru_act_sqrt(m):
            s_, s_k = LT(m % NSETS[0], 's')
            P.op('act', lambda e: e.activation(out=s_, in_=s_, func=AF.Sqrt, scale=-0.25, bias=dc(D_Q)), R=s_k + ['der'], W=s_k)
            yield

        def lru_dve(m, CAR, carname, main=False):
            b = m % NSETS[0]
            cv, cv_k = LT(b, 'cv')
            r_, r_k = LT(b, 'r')
            i_, i_k = LT(b, 'i')
            s_, s_k = LT(b, 's')
            hs, hs_k = LT(b, 'hs')
            P.op('dve', lambda e: e.scalar_tensor_tensor(out=i_, in0=i_, scalar=1.0, in1=cv, op0=ALU.add, op1=ALU.mult), R=i_k + cv_k, W=i_k)
            yield
            P.op('dve', lambda e: e.tensor_tensor(out=i_, in0=i_, in1=s_, op=ALU.mult), R=i_k + s_k, W=i_k)
            yield
            P.op('dve', lambda e: e.tensor_tensor_scan(out=hs, data0=r_, data1=i_, initial=CAR[:, m:m + 1], op0=ALU.mult, op1=ALU.add),
                 R=r_k + i_k + [(carname, m)], W=hs_k)
            yield
            P.op('dve', lambda e: e.tensor_copy(out=CAR[:, m:m + 1], in_=hs[:, T - 1:T]), R=hs_k, W=[(carname, m)])
            yield
            if main:
                P.op('dve', lambda e: e.tensor_tensor(out=HGc(m), in0=HGc(m), in1=hs, op=ALU.mult), R=HGk(m) + hs_k, W=HGk(m))
                yield

        def gate_mm(m, rbs):
            b = m % 2
            bk = 2 + b
            rb = rbs[m // 4]

            def mm(e):
                for kc in range(8):
                    ins = e.matmul(PS[:, bk, :], lhsT=WS(rb, kc, (m % 4) * 128), rhs=XNc(0, kc), start=(kc == 0), stop=(kc == 7))
                return ins
            P.op('pe', mm, R=[k for c in range(8) for k in XNk(0, c)] + wr_keys(rb), W=[('ps', bk)])

        def rr(gens):
            gens = list(gens)
            while gens:
                for g in list(gens):
                    try:
                        next(g)
                    except StopIteration:
                        gens.remove(g)

        def lru_stages(rec_rbs, gate_rbs, RH, CAR, rhname, carname, main):
            def st_front(ms):
                for m in ms:
                    rec_mm(m, rec_rbs)
                if main:
                    for m in ms:
                        gate_mm(m, gate_rbs)
                rr(lru_A(m, RH, rhname) for m in ms)
                if main:
                    for m in ms:
                        bk = 2 + (m % 2)
                        P.op('act', lambda e, m=m, bk=bk: e.activation(out=HGc(m), in_=PS[:, bk, :], func=AF.Gelu_apprx_tanh), R=[('ps', bk)], W=HGk(m))

            def st_exp(ms):
                rr(lru_gates_mm(m) for m in ms)
                rr(lru_act_exp(m) for m in ms)

            def st_sqrt(ms):
                rr(lru_act_sqrt(m) for m in ms)

            def st_dve(ms):
                rr(lru_dve(m, CAR, carname, main) for m in ms)
            return [st_front, st_exp, st_sqrt, st_dve]

        PAIRS = [(0, 1), (2, 3), (4, 5), (6, 7)]

        def lru_pairs(rec_rbs, gate_rbs, RH, CAR, rhname, carname, main):
            LT_MODE[0] = 'p1'
            NSETS[0] = 4
            st_front, st_exp, st_sqrt, st_dve = lru_stages(rec_rbs, gate_rbs, RH, CAR, rhname, carname, main)
            st_front(PAIRS[0])
            for k, ms in enumerate(PAIRS):
                if k + 1 < 4:
                    st_front(PAIRS[k + 1])
                st_exp(ms)
                st_sqrt(ms)
                st_dve(ms)
            LT_MODE[0] = 'main'
            NSETS[0] = 2

        def layer0_mixer(b, rbs4, RH, CAR, rhname, carname):
            norm_stats(b, 0)
            norm_apply(b, 0, 0, C_ANORM)
            lru_pairs(rbs4[2:4], rbs4[0:2], RH, CAR, rhname, carname, True)
            for i in range(2):
                rb = load_slot(S_WOUT + i)
                for mi in range(4):
                    m = i * 4 + mi
                    bk = m % 2

                    def mm(e, rb=rb, mi=mi, bk=bk):
                        for kc in range(8):
                            ins = e.matmul(PS[:, bk, :], lhsT=WS(rb, kc, mi * 128), rhs=HGc(kc), start=(kc == 0), stop=(kc == 7))
                        return ins
                    P.op('pe', mm, R=[k for c in range(8) for k in HGk(c)] + wr_keys(rb), W=[('ps', bk)])
                    P.op('dve', lambda e, m=m, bk=bk: e.tensor_tensor(out=Xc(b, m), in0=PS[:, bk, :], in1=Xc(b, m), op=ALU.add),
                         R=[('ps', bk)] + Xk(b, m), W=Xk(b, m))

        def pass1_tile(b, t):
            norm_stats(b, 0)
            norm_apply(b, 0, 0, C_ANORM)
            lru_pairs((2, 3), None, RH1, CAR1, 'rh1', 'car1', False)

        def mlp(b, layer, s_up, s_dn, store_t0=None):
            norm_stats(b, 0)
            norm_apply(b, 0, 0, C_MLP0 if layer == 0 else C_MLP1)
            ubanks = (0, 1, 6, 7)
            nu = 0
            for i in range(8):
                rb = load_slot(s_up + i)
                for fi in range(4):
                    f = i * 4 + fi
                    bk = ubanks[nu % 4]
                    nu += 1
                    rt, rt_k = RTb(f % 2)

                    def mm(e, rb=rb, fi=fi, bk=bk):
                        for kc in range(8):
                            ins = e.matmul(PS[:, bk, :], lhsT=WS(rb, kc, fi * 128), rhs=XNc(0, kc), start=(kc == 0), stop=(kc == 7))
                        return ins
                    P.op('pe', mm, R=[k for c in range(8) for k in XNk(0, c)] + wr_keys(rb), W=[('ps', bk)])
                    P.op('act', lambda e, bk=bk, rt=rt: e.activation(out=rt, in_=PS[:, bk, :], func=AF.Square), R=[('ps', bk)], W=rt_k)
                    P.op('dve', lambda e, bk=bk, rt=rt, f=f: e.scalar_tensor_tensor(out=HIDc(f), in0=PS[:, bk, :], scalar=0.0, in1=rt,
                                                                                   op0=ALU.is_gt, op1=ALU.mult),
                         R=[('ps', bk)] + rt_k, W=HIDk(f))
            for ch in range(2):
                for kg in range(4):
                    rb = load_slot(s_dn + ch * 4 + kg)
                    for mi in range(4):
                        def mm(e, rb=rb, mi=mi, kg=kg):
                            for kc in range(8):
                                ins = e.matmul(PS[:, 2 + mi, :], lhsT=WS(rb, kc, mi * 128), rhs=HIDc(kg * 8 + kc),
                                               start=(kg == 0 and kc == 0), stop=(kg == 3 and kc == 7))
                            return ins
                        P.op('pe', mm, R=[k for kc in range(8) for k in HIDk(kg * 8 + kc)] + wr_keys(rb), W=[('ps', 2 + mi)])
                for mi in range(4):
                    m = ch * 4 + mi
                    P.op('dve', lambda e, m=m, mi=mi: e.tensor_tensor(out=Xc(b, m), in0=PS[:, 2 + mi, :], in1=Xc(b, m), op=ALU.add),
                         R=[('ps', 2 + mi)] + Xk(b, m), W=Xk(b, m))
            if store_t0 is not None:
                keys = [k for c in range(8) for k in Xk(b, c)]
                src = av(O_X[b], 8 * T, F32).rearrange("p (c t) -> p c t", c=8)
                P.op('pool', lambda e: e.dma_start(out=ov[:, :, store_t0:store_t0 + T], in_=src), R=keys, W=[('out', store_t0)], dma='st%d' % b)

        def head_norm(bk, gcol, dst, dst_k, j, qm=None):
            kt, kt_k = KTb(j)
            sq = SQc(j)
            P.op('act', lambda e: e.activation(out=sq, in_=PS[:, bk, :], func=AF.Square), R=[('ps', bk)], W=SQk(j))
            P.op('pe', lambda e: e.matmul(PS[:, 6 + j, :], lhsT=BONES, rhs=sq, start=True, stop=True), R=SQk(j) + ['mats'], W=[('ps', 6 + j)])
            P.op('act', lambda e: e.activation(out=kt, in_=PS[:, 6 + j, :], func=AF.Ln, scale=1.0 / 64, bias=dc(D_EPS)),
                 R=[('ps', 6 + j), 'der'], W=kt_k)
            P.op('act', lambda e: e.activation(out=kt, in_=kt, func=AF.Exp, scale=-0.5), R=kt_k, W=kt_k)
            if qm is None:
                P.op('dve', lambda e: e.scalar_tensor_tensor(out=dst, in0=PS[:, bk, :], scalar=cc(gcol), in1=kt, op0=ALU.mult, op1=ALU.mult),
                     R=[('ps', bk)] + kt_k + ['cst'], W=dst_k)
            else:
                for hh in range(2):
                    lo_, hi_ = hh * 64, hh * 64 + 64
                    P.op('dve', lambda e, hh=hh, lo_=lo_, hi_=hi_: e.scalar_tensor_tensor(
                        out=QBD[lo_:hi_, qm, :, hh, :], in0=PS[lo_:hi_, bk, :].rearrange("p (a q) -> p a q", a=4), scalar=CST[lo_:hi_, gcol:gcol + 1],
                        in1=kt[lo_:hi_, :].rearrange("p (a q) -> p a q", a=4), op0=ALU.mult, op1=ALU.mult),
                        R=[('ps', bk)] + kt_k + ['cst'], W=QBDk(qm))

        ATT_FINE = [(n, j) for n in ('pexp', 'pm', 'osb', 'ktb', 'rdb', 'S') for j in range(2)] + [('ps', bnk) for bnk in range(2, 7)]

        def att_barrier():
            P.op('dve', lambda e: e.memset(TSJ[:, 0:1], 0.0), W=ATT_PAGES + ATT_FINE)

        def kv_phase(b, half, halo):
            att_barrier()
            norm_stats(b, 0)
            norm_apply(b, 0, 0, C_KVN)
            if not halo:
                norm_apply(b, 0, 1, C_BN)
            nk = 0
            for i in range(2):
                rb = load_slot(S_KV + i)
                for mi in range(4):
                    m = i * 4 + mi
                    bk = nk % 2
                    nk += 1

                    def mm(e, rb=rb, mi=mi, bk=bk):
                        for kc in range(8):
                            ins = e.matmul(PS[:, bk, :], lhsT=WS(rb, kc, mi * 128), rhs=XNc(0, kc), start=(kc == 0), stop=(kc == 7))
                        return ins
                    P.op('pe', mm, R=[k for c in range(8) for k in XNk(0, c)] + wr_keys(rb), W=[('ps', bk)])
                    head_norm(bk, C_GK, KR[:, half, m, :], [('K', half, m)], bk)
            for i in range(2):
                rb = load_slot(S_KV + 2 + i)
                for kt in range(4):
                    bk = nk % 2
                    nk += 1

                    def mm(e, rb=rb, kt=kt, bk=bk):
                        for kc in range(8):
                            ins = e.matmul(PS[:, bk, :], lhsT=XNc(0, kc)[:, kt * 128:(kt + 1) * 128], rhs=WS(rb, kc, 0, 512),
                                           start=(kc == 0), stop=(kc == 7))
                        return ins
                    P.op('pe', mm, R=[k for c in range(8) for k in XNk(0, c)] + wr_keys(rb), W=[('ps', bk)])
                    src = PS[:, bk, :].rearrange("p (h d) -> p h d", h=8)
                    dst = VR[:, half, kt, i * 8:(i + 1) * 8, 0:64]
                    if halo:
                        P.op('act', lambda e, src=src, dst=dst: e.activation(out=dst, in_=src, func=AF.Copy, scale=cc(C_HASPREV)),
                             R=[('ps', bk), 'cst'], W=[('V', half, kt, i)])
                    else:
                        P.op('act', lambda e, src=src, dst=dst: e.activation(out=dst, in_=src, func=AF.Copy),
                             R=[('ps', bk)], W=[('V', half, kt, i)])

        def attention(b, half):
            P.op('pool', lambda e: e.memset(QBD[0:64, :, :, 1, :].rearrange("p m a q -> p (m a) q"), 0.0), W=[k for m in range(8) for k in QBDk(m)])
            P.op('pool', lambda e: e.memset(QBD[64:128, :, :, 0, :].rearrange("p m a q -> p (m a) q"), 0.0),
                 R=[k for m in range(8) for k in QBDk(m)], W=[k for m in range(8) for k in QBDk(m)])
            nk = 0
            for i in range(2):
                rb = load_slot(S_Q + i)
                for mi in range(4):
                    m = i * 4 + mi
                    bk = nk % 2
                    nk += 1

                    def mm(e, rb=rb, mi=mi, bk=bk):
                        for kc in range(8):
                            ins = e.matmul(PS[:, bk, :], lhsT=WS(rb, kc, mi * 128), rhs=XNc(1, kc), start=(kc == 0), stop=(kc == 7))
                        return ins
                    P.op('pe', mm, R=[k for c in range(8) for k in XNk(1, c)] + wr_keys(rb), W=[('ps', bk)])
                    head_norm(bk, C_GQ, None, None, bk, qm=m)
            PSF = PS[:].rearrange("p a b -> p (a b)")
            PSB = PSF.bitcast(BF16)
            prev = 1 - half
            items = [(p, m) for p in range(4) for m in range(8)]
            SBASE = (2 * 512, 4 * 512 + 256)

            def win(p, w):
                kt = p + w
                return (prev, kt) if kt < 4 else (half, kt - 4)

            def qk(i):
                p, m = items[i]
                base = SBASE[i % 2]

                def mm(e):
                    for w in range(5):
                        hf, ktt = win(p, w)
                        ins = e.matmul(PSF[:, base + w * 256: base + (w + 1) * 256],
                                       lhsT=KR[:, hf, m, ktt * 128:(ktt + 1) * 128],
                                       rhs=QBD[:, m, p, :, :].rearrange("p h q -> p (h q)"), start=True, stop=True)
                    return ins
                P.op('pe', mm, R=[('K', prev, m), ('K', half, m)] + QBDk(m), W=[('S', i % 2)] + ([('ps', 6)] if i % 2 == 1 else []))

            qk(0)
            for i, (p, m) in enumerate(items):
                if i + 1 < len(items):
                    qk(i + 1)
                j = i % 2
                base = SBASE[j]
                g = m // 2
                obk = 7 if g % 2 == 0 else 0
                pe_, pe_k = PEb(j)
                pm_, pm_k = PMb(j)
                osb, osb_k = OSb(p % 2)
                P.op('act', lambda e, pe_=pe_, base=base: e.activation(out=pe_, in_=PSF[:, base: base + 1280], func=AF.Exp, scale=0.125),
                     R=[('S', j)], W=pe_k)
                P.op('dve', lambda e, pe_=pe_, pm_=pm_, m=m: e.tensor_tensor(out=pm_, in0=pe_, in1=EB[:, m].rearrange("p w h q -> p (w h q)"), op=ALU.mult),
                     R=pe_k + [('eb', 2 * m), ('eb', 2 * m + 1)], W=pm_k)
                for hh in range(2):
                    h = 2 * m + hh
                    hslot = h % 4

                    def mm2(e, p=p, h=h, hh=hh, hslot=hslot, obk=obk, pm_=pm_):
                        for w in range(5):
                            hf, ktt = win(p, w)
                            ins = e.matmul(PS[:, obk, hslot * 128: hslot * 128 + 65], lhsT=pm_[:, w * 256 + hh * 128: w * 256 + (hh + 1) * 128],
                                           rhs=VR[:, hf, ktt, h, 0:65], start=(w == 0), stop=(w == 4))
                        return ins
                    vkeys = [('V', hf_, kt_, h // 8) for hf_ in (0, 1) for kt_ in range(4)] + [('vones', 0), ('vones', 1)]
                    P.op('pe', mm2, R=pm_k + vkeys, W=[('ps', obk)])
                if m % 2 == 1:
                    rd, rd_k = RDb(g % 2)
                    P.op('dve', lambda e, obk=obk, rd=rd: e.reciprocal(out=rd[:, 0:4], in_=PS[:, obk, :].rearrange("p (h d) -> p h d", h=4)[:, :, 64]),
                         R=[('ps', obk)], W=rd_k)
                    for h2 in range(4):
                        hx = g * 4 + h2
                        P.op('act', lambda e, obk=obk, h2=h2, hx=hx, rd=rd, osb=osb: e.activation(
                            out=osb[:, hx * 64:(hx + 1) * 64], in_=PS[:, obk, h2 * 128: h2 * 128 + 64], func=AF.Identity, scale=rd[:, h2:h2 + 1]),
                            R=[('ps', obk)] + rd_k, W=osb_k)
                if m == 7:
                    tb = 1

                    def tr(e, osb=osb, tb=tb):
                        for m2 in range(8):
                            ins = e.transpose(PSB[:, tb * 1024 + m2 * 128: tb * 1024 + (m2 + 1) * 128], osb[:, m2 * 128:(m2 + 1) * 128], IDENT)
                        return ins
                    P.op('pe', tr, R=osb_k + ['mats'], W=[('ps', tb)])
                    for m2 in range(8):
                        P.op('dve', lambda e, m2=m2, tb=tb, p=p: e.tensor_copy(out=OTc(m2)[:, p * 128:(p + 1) * 128],
                                                                                in_=PSB[:, tb * 1024 + m2 * 128: tb * 1024 + (m2 + 1) * 128]),
                             R=[('ps', tb)], W=OTk(m2))
            for i in range(2):
                rb = load_slot(S_O + i)
                for mi in range(4):
                    m = i * 4 + mi
                    bk = m % 2

                    def mm(e, rb=rb, mi=mi, bk=bk):
                        for kc in range(8):
                            ins = e.matmul(PS[:, bk, :], lhsT=WS(rb, kc, mi * 128), rhs=OTc(kc), start=(kc == 0), stop=(kc == 7))
                        return ins
                    P.op('pe', mm, R=[k for c in range(8) for k in OTk(c)] + wr_keys(rb), W=[('ps', bk)])
                    P.op('dve', lambda e, m=m, bk=bk: e.tensor_tensor(out=Xc(b, m), in0=PS[:, bk, :], in1=Xc(b, m), op=ALU.add),
                         R=[('ps', bk)] + Xk(b, m), W=Xk(b, m))

        P.op('dve', lambda e: e.memset(RH1[:], 0.0), W=[('rh1', m) for m in range(8)])
        load_x(0, 0, q='sp')
        nprec_box = [0]
        LT_MODE[0] = 'p1'
        NSETS[0] = 8
        p1_stages = lru_stages((2, 3), None, RH1, CAR1, 'rh1', 'car1', False)
        gp = [(p, k) for p in range(NPRE) for k in range(4)]

        def prologue(p):
            nonlocal_n = nprec_box
            if p + 1 < NPRE:
                load_x((p + 1) % 2, (p + 1) * T, q='sp')
            for _ in range(2):
                if nonlocal_n[0] < NSLOT:
                    precast(nonlocal_n[0])
                    nonlocal_n[0] += 1
            norm_stats(p % 2, 0)
            norm_apply(p % 2, 0, p % 2, C_ANORM)
        prologue(0)
        nst = len(p1_stages)
        for g in range(len(gp) + nst - 1):
            for si in range(nst):
                idx = g - si
                if not (0 <= idx < len(gp)):
                    continue
                p, k = gp[idx]
                if si == 0 and k == 1 and p + 1 < NPRE:
                    prologue(p + 1)
                if si == 0:
                    XB[0] = p % 2
                p1_stages[si](PAIRS[k])
                if si == nst - 1 and k == 3:
                    P.op('dve', lambda e, p=p: e.tensor_scalar(out=CAR1[:], in0=CAR1[:], scalar1=cc(C_VALID + p), scalar2=None, op0=ALU.mult),
                         R=[('car1', m) for m in range(8)] + ['cst'], W=[('car1', m) for m in range(8)])
        LT_MODE[0] = 'main'
        NSETS[0] = 2
        XB[0] = 0
        while nprec_box[0] < NSLOT:
            precast(nprec_box[0])
            nprec_box[0] += 1

        ring_pages = pk(O_U + 40960, 8192 + 16384 + 16896)
        ring_keys = [('K', hf, m) for hf in range(2) for m in range(8)] + [('V', hf, kt, i) for hf in range(2) for kt in range(4) for i in range(2)] \
            + [('vones', 0), ('vones', 1)] + [k for c in range(8) for k in HGk(c)]
        P.op('dve', lambda e: e.memset(TSJ[:, 0:1], 0.0), W=ring_pages + ring_keys)
        for hf in range(2):
            P.op('dve', lambda e, hf=hf: e.memset(vones(hf), 1.0), W=[('vones', hf)])
        P.op('dve', lambda e: e.tensor_scalar(out=vones(1), in0=vones(1), scalar1=cc(C_HASPREV), scalar2=None, op0=ALU.mult),
             R=['cst', ('vones', 1)], W=[('vones', 1)])

        X0COL = NPRE * T

        def layer0(b):
            rbs4 = [load_slot(S_WIN + i) for i in range(4)]
            layer0_mixer(b, rbs4, RH1, CAR1, 'rh1', 'car1')

        load_x(0, X0COL)
        load_x(1, X0COL + T)
        layer0(0)
        mlp(0, 0, S_UP0, S_DN0)
        kv_phase(0, 1, True)
        att_barrier()
        P.op('dve', lambda e: e.tensor_scalar(out=CAR1[:], in0=CAR1[:], scalar1=cc(C_HASPREV), scalar2=None, op0=ALU.mult),
             R=[('car1', m) for m in range(8)] + ['cst'], W=[('car1', m) for m in range(8)])
        for t in range(NT):
            b = (t + 1) % 2
            half = t % 2
            if t + 1 < NT:
                load_x(t % 2, X0COL + T + (t + 1) * T)
            if t == 1:
                P.op('dve', lambda e: e.memset(vones(1), 1.0), W=[('vones', 1)])
            layer0(b)
            mlp(b, 0, S_UP0, S_DN0)
            kv_phase(b, half, False)
            attention(b, half)
            att_barrier()
            mlp(b, 1, S_UP1, S_DN1, store_t0=t * T)
        outkeys = [('out', t * T) for t in range(NT)]
        P.op('pool', None, R=outkeys)
        P.emit(nc)
    return nc


def _bias_table(rel_bias):
    j = np.arange(640)[:, None]
    q = np.arange(128)[None, :]
    u = q // 64
    qi = q % 64
    kj = j - 64 * u
    valid = (kj >= 0) & (kj < 576)
    dist = qi + 512 - kj
    idx = np.clip(dist, -63, 128) + 63
    tab = rel_bias[:, idx]
    tab = np.where(valid[None], tab, np.float32(-30000.0)).astype(np.float32)
    tab = tab.reshape(NH, 5, 128, 128)
    return np.ascontiguousarray(tab.transpose(2, 0, 1, 3).reshape(128, NH, 640))


def _prep_inputs(x, a_norm, a_w_in, a_conv_w, a_conv_b, a_w_gate, a_b_gate, a_lambda, a_w_out, kv_norm, w_kv,
                 k_norm, b_norm, b_w_q, b_q_norm, b_rel_bias, b_w_o, mlp_norm, w_up, w_down):
    f32 = np.float32
    x = np.asarray(x, f32)
    chunked = lambda v: np.asarray(v, f32).reshape(8, 128).T
    cst = np.zeros((128, NCST), f32)
    cst[:, C_ANORM:C_ANORM + 8] = chunked(a_norm[0])
    for k in range(4):
        cst[:, C_CONVW + 8 * k:C_CONVW + 8 * k + 8] = chunked(a_conv_w[0, k])
    cst[:, C_CONVB:C_CONVB + 8] = chunked(a_conv_b[0])
    bg = np.asarray(a_b_gate[0], f32)
    cst[:, C_BGR:C_BGR + 8] = bg[:, :128].T
    cst[:, C_BGI:C_BGI + 8] = bg[:, 128:].T
    cst[:, C_LAM:C_LAM + 8] = chunked(a_lambda[0])
    cst[:, C_MLP0:C_MLP0 + 8] = chunked(mlp_norm[0])
    cst[:, C_MLP1:C_MLP1 + 8] = chunked(mlp_norm[1])
    cst[:, C_KVN:C_KVN + 8] = chunked(kv_norm)
    cst[:, C_BN:C_BN + 8] = chunked(b_norm[0])
    cst[:, C_GK] = np.tile(np.asarray(k_norm, f32), 2)
    cst[:, C_GQ] = np.tile(np.asarray(b_q_norm[0], f32), 2)
    mats = np.zeros((128, 3, 128), f32)
    mats[:, 0, :] = 1.0
    mats[:64, 1, :64] = 1.0
    mats[64:, 1, 64:] = 1.0
    mats[:, 2, :] = np.eye(128, dtype=f32)
    relb = _bias_table(np.asarray(b_rel_bias[0], f32))
    shared = {
        "w_in": np.ascontiguousarray(a_w_in[0], f32), "w_gate": np.ascontiguousarray(a_w_gate[0], f32),
        "w_out": np.ascontiguousarray(a_w_out[0], f32), "w_kv": np.ascontiguousarray(w_kv, f32),
        "w_q": np.ascontiguousarray(b_w_q[0], f32), "w_o": np.ascontiguousarray(b_w_o[0], f32),
        "w_up": np.ascontiguousarray(w_up, f32), "w_down": np.ascontiguousarray(w_down, f32),
        "relb": relb, "mats": mats,
    }
    in_maps = []
    for j in range(NCORES):
        bi, s = j // 4, j % 4
        n = (s + 1) * SEG
        xt = np.zeros((D, TOK), f32)
        xt[:, TOK - n:] = x[bi, :n, :].T
        c = cst.copy()
        c[:, C_HASPREV] = 1.0 if s > 0 else 0.0
        for p in range(NPRE):
            c[:, C_VALID + p] = 1.0 if p >= (NPRE + 1) - 8 * s else 0.0
        m = dict(shared)
        m["xT"] = xt
        m["cst"] = c
        in_maps.append(m)
    return in_maps


_NC_CACHE = {}


def kernel(**inputs):
    in_maps = _prep_inputs(**inputs)
    if "nc" not in _NC_CACHE:
        _NC_CACHE["nc"] = build_nc()
    res = run_bass_kernel_spmd(_NC_CACHE["nc"], in_maps, core_ids=list(range(NCORES)))
    out = np.zeros((2, 4 * SEG, D), np.float32)
    for j in range(NCORES):
        bi, s = j // 4, j % 4
        out[bi, s * SEG:(s + 1) * SEG, :] = res.results[j]["outT"].T
    return out
```

```python
import contextlib
import numpy as np
import concourse.bass as bass
import concourse.mybir as mybir
from concourse.bass_utils import run_bass_kernel_spmd

F32 = mybir.dt.float32
BF16 = mybir.dt.bfloat16
U8 = mybir.dt.uint8
AF = mybir.ActivationFunctionType
ALU = mybir.AluOpType
AX = mybir.AxisListType

NCORES = 8
D = 1024
T = 512
NT = 8
SEG = T * NT
NPRE = 23
TOK = NPRE * T + T + SEG
NH = 16
EPS = 1e-6
NSLOT = 46


class Prog:
    ENGS = ('pe', 'act', 'dve', 'pool', 'sp')

    def __init__(self):
        self.ops = {e: [] for e in self.ENGS}
        self.res = {}
        self.seen = {e: {} for e in self.ENGS}
        self.dma_count = {}
        self.dma_inc = {}

    def op(self, eng, fn, R=(), W=(), dma=None, inc=16):
        deps = []
        for k in R:
            st = self.res.get(k)
            if st is not None and st[0] is not None:
                deps.append(st[0])
        for k in W:
            st = self.res.get(k)
            if st is not None:
                if st[0] is not None:
                    deps.append(st[0])
                deps.extend(st[1].items())
        idx = len(self.ops[eng])
        waits = {}
        seen = self.seen[eng]
        for (sk, v) in deps:
            if sk == 'pe' and eng == 'pe':
                continue
            if seen.get(sk, -1) >= v:
                continue
            if waits.get(sk, -1) < v:
                waits[sk] = v
        for sk, v in waits.items():
            seen[sk] = v
        if dma is None:
            ev = (eng, idx)
        else:
            dma = dma + '_' + eng
            c = self.dma_count.get(dma, 0) + 1
            self.dma_count[dma] = c
            self.dma_inc[dma] = inc
            ev = ('dma:' + dma, c)
        self.ops[eng].append(dict(fn=fn, waits=waits, flag=False, dma=dma))
        for k in R:
            st = self.res.setdefault(k, [None, {}])
            if st[1].get(ev[0], -1) < ev[1]:
                st[1][ev[0]] = ev[1]
        for k in W:
            self.res[k] = [ev, {}]
        return ev

    def emit(self, nc):
        ops = self.ops
        for e in self.ENGS:
            for o in ops[e]:
                for sk, v in o['waits'].items():
                    if not sk.startswith('dma:'):
                        ops[sk][v]['flag'] = True
        for e in self.ENGS:
            c = 0
            for o in ops[e]:
                if o['flag']:
                    c += 1
                o['rank'] = c
        with contextlib.ExitStack() as st:
            sems = {}
            for e in self.ENGS:
                sems[e] = st.enter_context(nc.semaphore('s_' + e))
            for k in self.dma_count:
                sems['dma:' + k] = st.enter_context(nc.semaphore('d_' + k))
            block = st.enter_context(nc.Block())

            def replay(ename):
                def body(eng):
                    for o in ops[ename]:
                        for sk, v in o['waits'].items():
                            if sk.startswith('dma:'):
                                eng.wait_ge(sems[sk], self.dma_inc[sk[4:]] * v)
                            else:
                                eng.wait_ge(sems[sk], ops[sk][v]['rank'])
                        if o['fn'] is None:
                            continue
                        ins = o['fn'](eng)
                        if o['dma'] is not None:
                            ins.then_inc(sems['dma:' + o['dma']], self.dma_inc[o['dma']])
                        elif o['flag']:
                            ins.then_inc(sems[ename], 1)
                return body

            block.tensor(replay('pe'))
            block.scalar(replay('act'))
            block.vector(replay('dve'))
            block.gpsimd(replay('pool'))
            block.sync(replay('sp'))
        return nc


C_ANORM, C_CONVW, C_CONVB, C_BGR, C_BGI, C_LAM = 0, 8, 40, 48, 56, 64
C_MLP0, C_MLP1, C_KVN, C_BN = 72, 80, 88, 96
C_GK, C_GQ, C_HASPREV = 104, 105, 106
C_VALID = 107
NCST = C_VALID + NPRE


def build_nc():
    nc = bass.Bass("TRN2", target_bir_lowering=False)
    dten = nc.dram_tensor
    xT = dten("xT", [D, TOK], F32, kind="ExternalInput").ap()
    w_in = dten("w_in", [D, 2 * D], F32, kind="ExternalInput").ap()
    w_gate = dten("w_gate", [8, 128, 256], F32, kind="ExternalInput").ap()
    w_out = dten("w_out", [D, D], F32, kind="ExternalInput").ap()
    w_kv = dten("w_kv", [D, 2 * D], F32, kind="ExternalInput").ap()
    w_q = dten("w_q", [D, D], F32, kind="ExternalInput").ap()
    w_o = dten("w_o", [D, D], F32, kind="ExternalInput").ap()
    w_up = dten("w_up", [2, D, 4 * D], F32, kind="ExternalInput").ap()
    w_down = dten("w_down", [2, 4 * D, D], F32, kind="ExternalInput").ap()
    cst_d = dten("cst", [128, NCST], F32, kind="ExternalInput").ap()
    relb_d = dten("relb", [128, NH, 640], F32, kind="ExternalInput").ap()
    mats_d = dten("mats", [128, 3, 128], F32, kind="ExternalInput").ap()
    outT = dten("outT", [D, SEG], F32, kind="ExternalOutput").ap()
    wscr = dten("wscr", [NSLOT, 128, 4096], BF16, kind="Internal").ap()

    xv = xT.rearrange("(c p) t -> p c t", p=128)
    ov = outT.rearrange("(c p) t -> p c t", p=128)

    def wsrc(w2d, r0, c0):
        return w2d[r0:r0 + 1024, c0:c0 + 512].rearrange("(kc p) j -> p kc j", p=128)
    slot_src = []
    for i in range(4):
        slot_src.append(wsrc(w_in, 0, i * 512))
    for i in range(2):
        slot_src.append(wsrc(w_out, 0, i * 512))
    for i in range(8):
        slot_src.append(wsrc(w_up[0], 0, i * 512))
    for ch in range(2):
        for kg in range(4):
            slot_src.append(wsrc(w_down[0], kg * 1024, ch * 512))
    for i in range(4):
        slot_src.append(wsrc(w_kv, 0, i * 512))
    for i in range(2):
        slot_src.append(wsrc(w_q, 0, i * 512))
    for i in range(2):
        slot_src.append(wsrc(w_o, 0, i * 512))
    for i in range(8):
        slot_src.append(wsrc(w_up[1], 0, i * 512))
    for ch in range(2):
        for kg in range(4):
            slot_src.append(wsrc(w_down[1], kg * 1024, ch * 512))
    assert len(slot_src) == NSLOT
    S_WIN, S_WOUT, S_UP0, S_DN0, S_KV, S_Q, S_O, S_UP1, S_DN1 = 0, 4, 6, 14, 22, 26, 28, 30, 38

    P = Prog()
    with contextlib.ExitStack() as st:
        sb = lambda name, shape, dt: st.enter_context(nc.sbuf_tensor(name, shape, dt))
        CST = sb("cstt", [128, NCST], F32)
        DER = sb("der", [128, 64], F32)
        MATS = sb("matsb", [128, 3, 128], BF16)
        WG = sb("wg", [128, 8, 256], F32)
        EB = sb("eb", [128, 8, 5, 2, 128], BF16)
        WR = sb("wr", [128, 4, 4096], BF16)
        RH1 = sb("rh1", [128, 8, 4], F32)
        CAR1 = sb("car1", [128, 8], F32)
        TSJ = sb("tsj", [128, 8], F32)
        ARENA_BYTES = 108 * 1024 + 16384 + 16896
        AR = sb("arena", [128, ARENA_BYTES], U8)
        PS = st.enter_context(nc.psum_tensor("ps", [128, 8, 512], F32))

        O_KR = 108 * 1024
        O_VR = O_KR + 16384
        KR = AR[:, O_KR:O_KR + 16384].bitcast(BF16).rearrange("p (a m t) -> p a m t", a=2, m=8)
        VR = AR[:, O_VR:O_VR + 16896].bitcast(BF16).rearrange("p (a k h d) -> p a k h d", a=2, k=4, h=NH)
        ONES = MATS[:, 0, :]
        BONES = MATS[:, 1, :]
        IDENT = MATS[:, 2, :]

        def av(off, n, dt):
            nb = n * (4 if dt == F32 else 2)
            return AR[:, off:off + nb].bitcast(dt)

        def pk(off, nbytes):
            return [('ar', i) for i in range(off // 1024, (off + nbytes + 1023) // 1024)]

        O_X = [0, 16384]
        O_XN = [32768, 40960]
        O_SQ = 49152
        O_RS = [57344, 59392]
        O_U = 61440

        def Xc(b, c, n=T):
            return av(O_X[b] + c * 2048, T, F32)[:, 0:n]

        def Xk(b, c):
            return pk(O_X[b] + c * 2048, 2048)

        def XNc(b, c, n=T):
            return av(O_XN[b] + c * 1024, T, BF16)[:, 0:n]

        def XNk(b, c):
            return pk(O_XN[b] + c * 1024, 1024)

        def SQc(c, n=T):
            return av(O_SQ + c * 1024, T, BF16)[:, 0:n]

        def SQk(c):
            return pk(O_SQ + c * 1024, 1024)

        def RSb(j, n=T):
            return av(O_RS[j], T, F32)[:, 0:n]

        def RSk(j):
            return pk(O_RS[j], 2048)

        LT_MODE = ['main']

        def LT(b, name):
            if LT_MODE[0] == 'p1':
                base = (O_U + b * 10240) if b < 4 else (O_U + 40960 + (b - 4) * 10240)
                offs = dict(rec=0, s=0, cv=3072, hs=3072, r=5120, i=7168, cvb=9216)
            else:
                base = O_U + b * 16384
                offs = dict(rec=0, cv=3072, r=5120, i=7168, s=9216, hs=11264, u=13312, cvb=15360)
            o = base + offs[name]
            if name == 'rec':
                return av(o, 516, F32), pk(o, 3072)
            if name == 'cvb':
                return av(o, T, BF16), pk(o, 1024)
            if name == 's' and LT_MODE[0] == 'p1':
                return av(o, T, F32), pk(o, 3072)
            return av(o, T, F32), pk(o, 2048)
        O_HG = O_U + 40960

        def HGc(c):
            return av(O_HG + c * 1024, T, BF16)

        def HGk(c):
            return pk(O_HG + c * 1024, 1024)

        def HIDc(f):
            return av(O_U + f * 1024, T, BF16)

        def HIDk(f):
            return pk(O_U + f * 1024, 1024)

        def RTb(j):
            o = O_U + 32768 + j * 2048
            return av(o, T, F32), pk(o, 2048)

        O_QBD = O_U
        O_OT = O_U + 16384
        O_PE = O_U + 24576
        O_PM = O_PE + 2 * 2560
        O_OS = O_PM + 2 * 2560
        O_KT = O_OS + 4096
        O_RD = O_KT + 4096

        QBD = av(O_QBD, 8192, BF16).rearrange("p (m a h q) -> p m a h q", m=8, a=4, h=2)

        def QBDk(m):
            return pk(O_QBD + m * 2048, 2048)

        def OTc(m):
            return av(O_OT + m * 1024, T, BF16)

        def OTk(m):
            return pk(O_OT + m * 1024, 1024)

        def PEb(j):
            return av(O_PE + j * 2560, 1280, BF16), [('pexp', j)]

        def PMb(j):
            return av(O_PM + j * 2560, 1280, BF16), [('pm', j)]

        def OSb(j):
            return av(O_OS + j * 2048, 1024, BF16), [('osb', j)]

        def KTb(j):
            return av(O_KT + j * 2048, T, F32), [('ktb', j)]

        def RDb(j):
            return av(O_RD + j * 64, 16, F32), [('rdb', j)]
        ATT_PAGES = pk(O_PE, O_RD + 128 - O_PE)

        cc = lambda col, n=1: CST[:, col:col + n]
        dc = lambda col, n=1: DER[:, col:col + n]
        D_EPS, D_Q, D_ONE = 0, 1, 2
        D_HBGR, D_HBGI, D_C, D_HC, D_T1 = 8, 16, 24, 32, 56

        P.op('sp', lambda e: e.dma_start(out=CST[:], in_=cst_d), W=['cst'], dma='ld0')
        P.op('pool', lambda e: e.dma_start(out=MATS[:], in_=mats_d), W=['mats'], dma='ld1')
        P.op('sp', lambda e: e.dma_start(out=WG[:], in_=w_gate.rearrange("n d e -> d n e")), W=['wg'], dma='ld2')
        P.op('dve', lambda e: e.memset(DER[:, 0:1], EPS), W=['der'])
        P.op('dve', lambda e: e.memset(DER[:, 1:2], 0.25), R=['der'], W=['der'])
        P.op('dve', lambda e: e.memset(DER[:, 2:3], 1.0), R=['der'], W=['der'])
        P.op('dve', lambda e: e.memset(CAR1[:], 0.0), W=[('car1', m) for m in range(8)])
        P.op('dve', lambda e: e.memset(TSJ[:], 0.0), W=['tsj'])
        P.op('dve', lambda e: e.tensor_scalar(out=dc(D_HBGR, 16), in0=cc(C_BGR, 16), scalar1=0.5, scalar2=None, op0=ALU.mult),
             R=['cst', 'der'], W=['der'])
        P.op('act', lambda e: e.activation(out=dc(D_T1, 8), in_=cc(C_LAM, 8), func=AF.Exp, scale=-1.0), R=['cst', 'der'], W=['der'])
        P.op('act', lambda e: e.activation(out=dc(D_T1, 8), in_=dc(D_T1, 8), func=AF.Ln, bias=dc(D_ONE), scale=1.0), R=['der'], W=['der'])
        P.op('dve', lambda e: e.tensor_scalar(out=dc(D_C, 8), in0=dc(D_T1, 8), scalar1=-8.0, scalar2=None, op0=ALU.mult), R=['der'], W=['der'])
        P.op('dve', lambda e: e.tensor_scalar(out=dc(D_HC, 8), in0=dc(D_T1, 8), scalar1=-4.0, scalar2=None, op0=ALU.mult), R=['der'], W=['der'])
        for h in range(NH):
            stg = av(O_U + (h % 2) * 4096, 640, F32)
            sk = pk(O_U + (h % 2) * 4096, 2560)
            P.op('sp', lambda e, h=h, stg=stg: e.dma_start(out=stg, in_=relb_d[:, h, :]), W=sk, dma='ldb%d' % (h % 2))
            P.op('act', lambda e, h=h, stg=stg: e.activation(out=EB[:, h // 2, :, h % 2, :], in_=stg.rearrange("p (w q) -> p w q", w=5), func=AF.Exp), R=sk, W=[('eb', h)])
        def vones(hf):
            return VR[:, hf].rearrange("p k h d -> p (k h) d")[:, :, 64:66]

        ring_pos = [0]

        def wr_keys(rb):
            return [('wr', rb)]

        def load_slot(slot):
            rb = ring_pos[0] % 4
            ring_pos[0] += 1
            P.op('sp', lambda e, rb=rb, slot=slot: e.dma_start(out=WR[:, rb, :], in_=wscr[slot]),
                 R=[('scr', slot)], W=wr_keys(rb), dma='w%d' % rb)
            return rb

        def WS(rb, kc, c0, n=128):
            return WR[:, rb, kc * 512 + c0: kc * 512 + c0 + n]

        for i in (2, 3):
            P.op('pool', lambda e, i=i: e.dma_start(out=WR[:, i, :].rearrange("p (kc j) -> p kc j", kc=8), in_=slot_src[i]),
                 W=wr_keys(i), dma='w%d' % i)
        pc_cnt = [0]

        def precast(s):
            stg = pc_cnt[0] % 2
            pc_cnt[0] += 1
            P.op('pool', lambda e, s=s, stg=stg: e.dma_start(out=WR[:, stg, :].rearrange("p (kc j) -> p kc j", kc=8), in_=slot_src[s]),
                 W=wr_keys(stg), dma='w%d' % stg)
            P.op('sp', lambda e, s=s, stg=stg: e.dma_start(out=wscr[s], in_=WR[:, stg, :]), R=wr_keys(stg), W=[('scr', s)], dma='pc%d' % (s % 4))

        def load_x(b, t0, n=T, q='pool'):
            keys = [k for c in range(8) for k in Xk(b, c)]
            dst = av(O_X[b], 8 * T, F32).rearrange("p (c t) -> p c t", c=8)[:, :, 0:n]
            P.op(q, lambda e: e.dma_start(out=dst, in_=xv[:, :, t0:t0 + n]), W=keys, dma='x%s%d' % (q, b))

        def norm_stats(b, j, n=T, scale=1.0 / D):
            for c in range(8):
                P.op('act', lambda e, c=c: e.activation(out=SQc(c, n), in_=Xc(b, c, n), func=AF.Square), R=Xk(b, c), W=SQk(c))

            def mm(e):
                for c in range(8):
                    ins = e.matmul(PS[:, 7, 0:n], lhsT=ONES, rhs=SQc(c, n), start=(c == 0), stop=(c == 7))
                return ins
            P.op('pe', mm, R=[k for c in range(8) for k in SQk(c)] + ['mats'], W=[('ps', 7)])
            P.op('act', lambda e: e.activation(out=RSb(j, n), in_=PS[:, 7, 0:n], func=AF.Ln, scale=scale, bias=dc(D_EPS)),
                 R=[('ps', 7), 'der'], W=RSk(j))
            P.op('act', lambda e: e.activation(out=RSb(j, n), in_=RSb(j, n), func=AF.Exp, scale=-0.5), R=RSk(j), W=RSk(j))

        def norm_apply(b, j, xb, gcol, n=T):
            for c in range(8):
                P.op('dve', lambda e, c=c: e.scalar_tensor_tensor(out=XNc(xb, c, n), in0=Xc(b, c, n), scalar=cc(gcol + c), in1=RSb(j, n),
                                                                  op0=ALU.mult, op1=ALU.mult),
                     R=Xk(b, c) + RSk(j) + ['cst'], W=XNk(xb, c))

        XB = [0]

        def rec_mm(m, rbs, n=T, bank=None):
            bk = (m % 2) if bank is None else bank
            rb = rbs[m // 4]
            xb = XB[0]

            def mm(e):
                for kc in range(8):
                    ins = e.matmul(PS[:, bk, 0:n], lhsT=WS(rb, kc, (m % 4) * 128), rhs=XNc(xb, kc, n), start=(kc == 0), stop=(kc == 7))
                return ins
            P.op('pe', mm, R=[k for c in range(8) for k in XNk(xb, c)] + wr_keys(rb), W=[('ps', bk)])
            return bk

        NSETS = [2]

        def lru_A(m, RH, rhname):
            b = m % NSETS[0]
            rec, rec_k = LT(b, 'rec')
            cv, cv_k = LT(b, 'cv')
            cvb, cvb_k = LT(b, 'cvb')
            bk = m % 2
            P.op('act', lambda e: e.activation(out=rec[:, 0:3], in_=RH[:, m, 0:3], func=AF.Copy), R=[(rhname, m)], W=rec_k)
            yield
            P.op('act', lambda e: e.activation(out=rec[:, 3:3 + T], in_=PS[:, bk, :], func=AF.Copy), R=[('ps', bk)] + rec_k, W=rec_k)
            yield
            P.op('act', lambda e: e.activation(out=RH[:, m, 0:3], in_=rec[:, T:T + 3], func=AF.Copy), R=rec_k, W=[(rhname, m)])
            yield
            P.op('dve', lambda e: e.tensor_scalar(out=cv, in0=rec[:, 0:T], scalar1=cc(C_CONVW + m), scalar2=cc(C_CONVB + m), op0=ALU.mult, op1=ALU.add),
                 R=rec_k + ['cst'], W=cv_k)
            yield
            for k in range(1, 4):
                P.op('dve', lambda e, k=k: e.scalar_tensor_tensor(out=cv, in0=rec[:, k:k + T], scalar=cc(C_CONVW + 8 * k + m), in1=cv,
                                                                  op0=ALU.mult, op1=ALU.add),
                     R=rec_k + cv_k + ['cst'], W=cv_k)
                yield

        def lru_gates_mm(m):
            b = m % 2
            cv, cv_k = LT(m % NSETS[0], 'cv')

            def mm(e):
                e.matmul(PS[:, 4 + b, :], lhsT=WG[:, m, 0:128], rhs=cv, start=True, stop=True)
                return e.matmul(PS[:, 6 + b, :], lhsT=WG[:, m, 128:256], rhs=cv, start=True, stop=True)
            P.op('pe', mm, R=cv_k + ['wg'], W=[('ps', 4 + b), ('ps', 6 + b)])
            yield

        def lru_act_exp(m):
            b = m % 2
            r_, r_k = LT(m % NSETS[0], 'r')
            i_, i_k = LT(m % NSETS[0], 'i')
            s_, s_k = LT(m % NSETS[0], 's')
            P.op('act', lambda e: e.activation(out=r_, in_=PS[:, 4 + b, :], func=AF.Tanh, scale=0.5, bias=dc(D_HBGR + m)),
                 R=[('ps', 4 + b), 'der'], W=r_k)
            yield
            P.op('act', lambda e: e.activation(out=i_, in_=PS[:, 6 + b, :], func=AF.Tanh, scale=0.5, bias=dc(D_HBGI + m)),
                 R=[('ps', 6 + b), 'der'], W=i_k)
            yield
            P.op('act', lambda e: e.activation(out=r_, in_=r_, func=AF.Exp, scale=dc(D_HC + m), bias=dc(D_HC + m)), R=r_k + ['der'], W=r_k)
            yield
            P.op('pool', lambda e: e.tensor_tensor(out=s_, in0=r_, in1=r_, op=ALU.mult), R=r_k, W=s_k)
            yield

        def lru_act_sqrt(m):
            s_, s_k = LT(m % NSETS[0], 's')
            P.op('act', lambda e: e.activation(out=s_, in_=s_, func=AF.Sqrt, scale=-0.25, bias=dc(D_Q)), R=s_k + ['der'], W=s_k)
            yield

        def lru_dve(m, CAR, carname, main=False):
            b = m % NSETS[0]
            cv, cv_k = LT(b, 'cv')
            r_, r_k = LT(b, 'r')
            i_, i_k = LT(b, 'i')
            s_, s_k = LT(b, 's')
            hs, hs_k = LT(b, 'hs')
            P.op('dve', lambda e: e.scalar_tensor_tensor(out=i_, in0=i_, scalar=1.0, in1=cv, op0=ALU.add, op1=ALU.mult), R=i_k + cv_k, W=i_k)
            yield
            P.op('dve', lambda e: e.tensor_tensor(out=i_, in0=i_, in1=s_, op=ALU.mult), R=i_k + s_k, W=i_k)
            yield
            P.op('dve', lambda e: e.tensor_tensor_scan(out=hs, data0=r_, data1=i_, initial=CAR[:, m:m + 1], op0=ALU.mult, op1=ALU.add),
                 R=r_k + i_k + [(carname, m)], W=hs_k)
            yield
            P.op('dve', lambda e: e.tensor_copy(out=CAR[:, m:m + 1], in_=hs[:, T - 1:T]), R=hs_k, W=[(carname, m)])
            yield
            if main:
                P.op('dve', lambda e: e.tensor_tensor(out=HGc(m), in0=HGc(m), in1=hs, op=ALU.mult), R=HGk(m) + hs_k, W=HGk(m))
                yield

        def gate_mm(m, rbs):
            b = m % 2
            bk = 2 + b
            rb = rbs[m // 4]
            xb = XB[0]

            def mm(e):
                for kc in range(8):
                    ins = e.matmul(PS[:, bk, :], lhsT=WS(rb, kc, (m % 4) * 128), rhs=XNc(xb, kc), start=(kc == 0), stop=(kc == 7))
                return ins
            P.op('pe', mm, R=[k for c in range(8) for k in XNk(xb, c)] + wr_keys(rb), W=[('ps', bk)])

        def rr(gens):
            gens = list(gens)
            while gens:
                for g in list(gens):
                    try:
                        next(g)
                    except StopIteration:
                        gens.remove(g)

        def lru_stages(rec_rbs, gate_rbs, RH, CAR, rhname, carname, main):
            def st_front(ms):
                for m in ms:
                    rec_mm(m, rec_rbs)
                if main:
                    for m in ms:
                        gate_mm(m, gate_rbs)
                rr(lru_A(m, RH, rhname) for m in ms)
                if main:
                    for m in ms:
                        bk = 2 + (m % 2)
                        P.op('act', lambda e, m=m, bk=bk: e.activation(out=HGc(m), in_=PS[:, bk, :], func=AF.Gelu_apprx_tanh), R=[('ps', bk)], W=HGk(m))

            def st_exp(ms):
                rr(lru_gates_mm(m) for m in ms)
                rr(lru_act_exp(m) for m in ms)

            def st_sqrt(ms):
                rr(lru_act_sqrt(m) for m in ms)

            def st_dve(ms):
                rr(lru_dve(m, CAR, carname, main) for m in ms)
            return [st_front, st_exp, st_sqrt, st_dve]

        PAIRS = [(0, 1), (2, 3), (4, 5), (6, 7)]

        def lru_pairs(rec_rbs, gate_rbs, RH, CAR, rhname, carname, main):
            LT_MODE[0] = 'p1'
            NSETS[0] = 4
            st_front, st_exp, st_sqrt, st_dve = lru_stages(rec_rbs, gate_rbs, RH, CAR, rhname, carname, main)
            st_front(PAIRS[0])
            for k, ms in enumerate(PAIRS):
                if k + 1 < 4:
                    st_front(PAIRS[k + 1])
                st_exp(ms)
                st_sqrt(ms)
                st_dve(ms)
            LT_MODE[0] = 'main'
            NSETS[0] = 2

        PRENORM = [False]

        def layer0_mixer(b, rbs4, RH, CAR, rhname, carname):
            if PRENORM[0]:
                XB[0] = 1
            else:
                norm_stats(b, 0)
                norm_apply(b, 0, 0, C_ANORM)
            lru_pairs(rbs4[2:4], rbs4[0:2], RH, CAR, rhname, carname, True)
            XB[0] = 0
            for i in range(2):
                rb = load_slot(S_WOUT + i)
                for mi in range(4):
                    m = i * 4 + mi
                    bk = m % 2

                    def mm(e, rb=rb, mi=mi, bk=bk):
                        for kc in range(8):
                            ins = e.matmul(PS[:, bk, :], lhsT=WS(rb, kc, mi * 128), rhs=HGc(kc), start=(kc == 0), stop=(kc == 7))
                        return ins
                    P.op('pe', mm, R=[k for c in range(8) for k in HGk(c)] + wr_keys(rb), W=[('ps', bk)])
                    P.op('dve', lambda e, m=m, bk=bk: e.tensor_tensor(out=Xc(b, m), in0=PS[:, bk, :], in1=Xc(b, m), op=ALU.add),
                         R=[('ps', bk)] + Xk(b, m), W=Xk(b, m))

        def pass1_tile(b, t):
            norm_stats(b, 0)
            norm_apply(b, 0, 0, C_ANORM)
            lru_pairs((2, 3), None, RH1, CAR1, 'rh1', 'car1', False)

        def mlp(b, layer, s_up, s_dn, store_t0=None):
            norm_stats(b, 0)
            norm_apply(b, 0, 0, C_MLP0 if layer == 0 else C_MLP1)
            ubanks = (0, 1, 6, 7)
            nu = 0
            for i in range(8):
                rb = load_slot(s_up + i)
                for fi in range(4):
                    f = i * 4 + fi
                    bk = ubanks[nu % 4]
                    nu += 1
                    rt, rt_k = RTb(f % 2)

                    def mm(e, rb=rb, fi=fi, bk=bk):
                        for kc in range(8):
                            ins = e.matmul(PS[:, bk, :], lhsT=WS(rb, kc, fi * 128), rhs=XNc(0, kc), start=(kc == 0), stop=(kc == 7))
                        return ins
                    P.op('pe', mm, R=[k for c in range(8) for k in XNk(0, c)] + wr_keys(rb), W=[('ps', bk)])
                    P.op('act', lambda e, bk=bk, rt=rt: e.activation(out=rt, in_=PS[:, bk, :], func=AF.Square), R=[('ps', bk)], W=rt_k)
                    P.op('dve', lambda e, bk=bk, rt=rt, f=f: e.scalar_tensor_tensor(out=HIDc(f), in0=PS[:, bk, :], scalar=0.0, in1=rt,
                                                                                   op0=ALU.is_gt, op1=ALU.mult),
                         R=[('ps', bk)] + rt_k, W=HIDk(f))
            for ch in range(2):
                for kg in range(4):
                    rb = load_slot(s_dn + ch * 4 + kg)
                    for mi in range(4):
                        def mm(e, rb=rb, mi=mi, kg=kg):
                            for kc in range(8):
                                ins = e.matmul(PS[:, 2 + mi, :], lhsT=WS(rb, kc, mi * 128), rhs=HIDc(kg * 8 + kc),
                                               start=(kg == 0 and kc == 0), stop=(kg == 3 and kc == 7))
                            return ins
                        P.op('pe', mm, R=[k for kc in range(8) for k in HIDk(kg * 8 + kc)] + wr_keys(rb), W=[('ps', 2 + mi)])
                for mi in range(4):
                    m = ch * 4 + mi
                    P.op('dve', lambda e, m=m, mi=mi: e.tensor_tensor(out=Xc(b, m), in0=PS[:, 2 + mi, :], in1=Xc(b, m), op=ALU.add),
                         R=[('ps', 2 + mi)] + Xk(b, m), W=Xk(b, m))
            if store_t0 is not None:
                keys = [k for c in range(8) for k in Xk(b, c)]
                src = av(O_X[b], 8 * T, F32).rearrange("p (c t) -> p c t", c=8)
                P.op('pool', lambda e: e.dma_start(out=ov[:, :, store_t0:store_t0 + T], in_=src), R=keys, W=[('out', store_t0)], dma='st%d' % b)

        def head_norm(bk, gcol, dst, dst_k, j, qm=None):
            kt, kt_k = KTb(j)
            sq = SQc(j)
            P.op('act', lambda e: e.activation(out=sq, in_=PS[:, bk, :], func=AF.Square), R=[('ps', bk)], W=SQk(j))
            P.op('pe', lambda e: e.matmul(PS[:, 6 + j, :], lhsT=BONES, rhs=sq, start=True, stop=True), R=SQk(j) + ['mats'], W=[('ps', 6 + j)])
            P.op('act', lambda e: e.activation(out=kt, in_=PS[:, 6 + j, :], func=AF.Ln, scale=1.0 / 64, bias=dc(D_EPS)),
                 R=[('ps', 6 + j), 'der'], W=kt_k)
            P.op('act', lambda e: e.activation(out=kt, in_=kt, func=AF.Exp, scale=-0.5), R=kt_k, W=kt_k)
            if qm is None:
                P.op('dve', lambda e: e.scalar_tensor_tensor(out=dst, in0=PS[:, bk, :], scalar=cc(gcol), in1=kt, op0=ALU.mult, op1=ALU.mult),
                     R=[('ps', bk)] + kt_k + ['cst'], W=dst_k)
            else:
                for hh in range(2):
                    lo_, hi_ = hh * 64, hh * 64 + 64
                    P.op('dve', lambda e, hh=hh, lo_=lo_, hi_=hi_: e.scalar_tensor_tensor(
                        out=QBD[lo_:hi_, qm, :, hh, :], in0=PS[lo_:hi_, bk, :].rearrange("p (a q) -> p a q", a=4), scalar=CST[lo_:hi_, gcol:gcol + 1],
                        in1=kt[lo_:hi_, :].rearrange("p (a q) -> p a q", a=4), op0=ALU.mult, op1=ALU.mult),
                        R=[('ps', bk)] + kt_k + ['cst'], W=QBDk(qm))

        ATT_FINE = [(n, j) for n in ('pexp', 'pm', 'osb', 'ktb', 'rdb', 'S') for j in range(2)] + [('ps', bnk) for bnk in range(2, 7)]

        def att_barrier():
            P.op('dve', lambda e: e.memset(TSJ[:, 0:1], 0.0), W=ATT_PAGES + ATT_FINE)

        def kv_phase(b, half, halo):
            att_barrier()
            norm_stats(b, 0)
            norm_apply(b, 0, 0, C_KVN)
            if not halo:
                norm_apply(b, 0, 1, C_BN)
            nk = 0
            for i in range(2):
                rb = load_slot(S_KV + i)
                for mi in range(4):
                    m = i * 4 + mi
                    bk = nk % 2
                    nk += 1

                    def mm(e, rb=rb, mi=mi, bk=bk):
                        for kc in range(8):
                            ins = e.matmul(PS[:, bk, :], lhsT=WS(rb, kc, mi * 128), rhs=XNc(0, kc), start=(kc == 0), stop=(kc == 7))
                        return ins
                    P.op('pe', mm, R=[k for c in range(8) for k in XNk(0, c)] + wr_keys(rb), W=[('ps', bk)])
                    head_norm(bk, C_GK, KR[:, half, m, :], [('K', half, m)], bk)
            for i in range(2):
                rb = load_slot(S_KV + 2 + i)
                for kt in range(4):
                    bk = nk % 2
                    nk += 1

                    def mm(e, rb=rb, kt=kt, bk=bk):
                        for kc in range(8):
                            ins = e.matmul(PS[:, bk, :], lhsT=XNc(0, kc)[:, kt * 128:(kt + 1) * 128], rhs=WS(rb, kc, 0, 512),
                                           start=(kc == 0), stop=(kc == 7))
                        return ins
                    P.op('pe', mm, R=[k for c in range(8) for k in XNk(0, c)] + wr_keys(rb), W=[('ps', bk)])
                    src = PS[:, bk, :].rearrange("p (h d) -> p h d", h=8)
                    dst = VR[:, half, kt, i * 8:(i + 1) * 8, 0:64]
                    if halo:
                        P.op('act', lambda e, src=src, dst=dst: e.activation(out=dst, in_=src, func=AF.Copy, scale=cc(C_HASPREV)),
                             R=[('ps', bk), 'cst'], W=[('V', half, kt, i)])
                    else:
                        P.op('act', lambda e, src=src, dst=dst: e.activation(out=dst, in_=src, func=AF.Copy),
                             R=[('ps', bk)], W=[('V', half, kt, i)])

        def attention(b, half):
            P.op('pool', lambda e: e.memset(QBD[0:64, :, :, 1, :].rearrange("p m a q -> p (m a) q"), 0.0), W=[k for m in range(8) for k in QBDk(m)])
            P.op('pool', lambda e: e.memset(QBD[64:128, :, :, 0, :].rearrange("p m a q -> p (m a) q"), 0.0),
                 R=[k for m in range(8) for k in QBDk(m)], W=[k for m in range(8) for k in QBDk(m)])
            nk = 0
            for i in range(2):
                rb = load_slot(S_Q + i)
                for mi in range(4):
                    m = i * 4 + mi
                    bk = nk % 2
                    nk += 1

                    def mm(e, rb=rb, mi=mi, bk=bk):
                        for kc in range(8):
                            ins = e.matmul(PS[:, bk, :], lhsT=WS(rb, kc, mi * 128), rhs=XNc(1, kc), start=(kc == 0), stop=(kc == 7))
                        return ins
                    P.op('pe', mm, R=[k for c in range(8) for k in XNk(1, c)] + wr_keys(rb), W=[('ps', bk)])
                    head_norm(bk, C_GQ, None, None, bk, qm=m)
            PSF = PS[:].rearrange("p a b -> p (a b)")
            PSB = PSF.bitcast(BF16)
            prev = 1 - half
            items = [(p, m) for p in range(4) for m in range(8)]
            SBASE = (2 * 512, 4 * 512 + 256)

            def win(p, w):
                kt = p + w
                return (prev, kt) if kt < 4 else (half, kt - 4)

            def qk(i):
                p, m = items[i]
                base = SBASE[i % 2]

                def mm(e):
                    for w in range(5):
                        hf, ktt = win(p, w)
                        ins = e.matmul(PSF[:, base + w * 256: base + (w + 1) * 256],
                                       lhsT=KR[:, hf, m, ktt * 128:(ktt + 1) * 128],
                                       rhs=QBD[:, m, p, :, :].rearrange("p h q -> p (h q)"), start=True, stop=True)
                    return ins
                P.op('pe', mm, R=[('K', prev, m), ('K', half, m)] + QBDk(m), W=[('S', i % 2)] + ([('ps', 6)] if i % 2 == 1 else []))

            qk(0)
            for i, (p, m) in enumerate(items):
                if i + 1 < len(items):
                    qk(i + 1)
                j = i % 2
                base = SBASE[j]
                g = m // 2
                obk = 7 if g % 2 == 0 else 0
                pe_, pe_k = PEb(j)
                pm_, pm_k = PMb(j)
                osb, osb_k = OSb(p % 2)
                P.op('act', lambda e, pe_=pe_, base=base: e.activation(out=pe_, in_=PSF[:, base: base + 1280], func=AF.Exp, scale=0.125),
                     R=[('S', j)], W=pe_k)
                P.op('dve', lambda e, pe_=pe_, pm_=pm_, m=m: e.tensor_tensor(out=pm_, in0=pe_, in1=EB[:, m].rearrange("p w h q -> p (w h q)"), op=ALU.mult),
                     R=pe_k + [('eb', 2 * m), ('eb', 2 * m + 1)], W=pm_k)
                for hh in range(2):
                    h = 2 * m + hh
                    hslot = h % 4

                    def mm2(e, p=p, h=h, hh=hh, hslot=hslot, obk=obk, pm_=pm_):
                        for w in range(5):
                            hf, ktt = win(p, w)
                            ins = e.matmul(PS[:, obk, hslot * 128: hslot * 128 + 65], lhsT=pm_[:, w * 256 + hh * 128: w * 256 + (hh + 1) * 128],
                                           rhs=VR[:, hf, ktt, h, 0:65], start=(w == 0), stop=(w == 4))
                        return ins
                    vkeys = [('V', hf_, kt_, h // 8) for hf_ in (0, 1) for kt_ in range(4)] + [('vones', 0), ('vones', 1)]
                    P.op('pe', mm2, R=pm_k + vkeys, W=[('ps', obk)])
                if m % 2 == 1:
                    rd, rd_k = RDb(g % 2)
                    P.op('dve', lambda e, obk=obk, rd=rd: e.reciprocal(out=rd[:, 0:4], in_=PS[:, obk, :].rearrange("p (h d) -> p h d", h=4)[:, :, 64]),
                         R=[('ps', obk)], W=rd_k)
                    for h2 in range(4):
                        hx = g * 4 + h2
                        P.op('act', lambda e, obk=obk, h2=h2, hx=hx, rd=rd, osb=osb: e.activation(
                            out=osb[:, hx * 64:(hx + 1) * 64], in_=PS[:, obk, h2 * 128: h2 * 128 + 64], func=AF.Identity, scale=rd[:, h2:h2 + 1]),
                            R=[('ps', obk)] + rd_k, W=osb_k)
                if m == 7:
                    tb = 1

                    def tr(e, osb=osb, tb=tb):
                        for m2 in range(8):
                            ins = e.transpose(PSB[:, tb * 1024 + m2 * 128: tb * 1024 + (m2 + 1) * 128], osb[:, m2 * 128:(m2 + 1) * 128], IDENT)
                        return ins
                    P.op('pe', tr, R=osb_k + ['mats'], W=[('ps', tb)])
                    for m2 in range(8):
                        P.op('dve', lambda e, m2=m2, tb=tb, p=p: e.tensor_copy(out=OTc(m2)[:, p * 128:(p + 1) * 128],
                                                                                in_=PSB[:, tb * 1024 + m2 * 128: tb * 1024 + (m2 + 1) * 128]),
                             R=[('ps', tb)], W=OTk(m2))
            for i in range(2):
                rb = load_slot(S_O + i)
                for mi in range(4):
                    m = i * 4 + mi
                    bk = m % 2

                    def mm(e, rb=rb, mi=mi, bk=bk):
                        for kc in range(8):
                            ins = e.matmul(PS[:, bk, :], lhsT=WS(rb, kc, mi * 128), rhs=OTc(kc), start=(kc == 0), stop=(kc == 7))
                        return ins
                    P.op('pe', mm, R=[k for c in range(8) for k in OTk(c)] + wr_keys(rb), W=[('ps', bk)])
                    P.op('dve', lambda e, m=m, bk=bk: e.tensor_tensor(out=Xc(b, m), in0=PS[:, bk, :], in1=Xc(b, m), op=ALU.add),
                         R=[('ps', bk)] + Xk(b, m), W=Xk(b, m))

        P.op('dve', lambda e: e.memset(RH1[:], 0.0), W=[('rh1', m) for m in range(8)])
        load_x(0, 0, q='sp')
        nprec_box = [0]
        LT_MODE[0] = 'p1'
        NSETS[0] = 8
        p1_stages = lru_stages((2, 3), None, RH1, CAR1, 'rh1', 'car1', False)
        gp = [(p, k) for p in range(NPRE) for k in range(4)]

        def prologue(p):
            nonlocal_n = nprec_box
            if p + 1 < NPRE:
                load_x((p + 1) % 2, (p + 1) * T, q='sp')
            for _ in range(2):
                if nonlocal_n[0] < NSLOT:
                    precast(nonlocal_n[0])
                    nonlocal_n[0] += 1
            norm_stats(p % 2, 0)
            norm_apply(p % 2, 0, p % 2, C_ANORM)
        prologue(0)
        nst = len(p1_stages)
        for g in range(len(gp) + nst - 1):
            for si in range(nst):
                idx = g - si
                if not (0 <= idx < len(gp)):
                    continue
                p, k = gp[idx]
                if si == 0 and k == 2 and p + 1 < NPRE:
                    prologue(p + 1)
                if si == 0:
                    XB[0] = p % 2
                p1_stages[si](PAIRS[k])
                if si == nst - 1 and k == 3:
                    P.op('dve', lambda e, p=p: e.tensor_scalar(out=CAR1[:], in0=CAR1[:], scalar1=cc(C_VALID + p), scalar2=None, op0=ALU.mult),
                         R=[('car1', m) for m in range(8)] + ['cst'], W=[('car1', m) for m in range(8)])
        LT_MODE[0] = 'main'
        NSETS[0] = 2
        XB[0] = 0
        while nprec_box[0] < NSLOT:
            precast(nprec_box[0])
            nprec_box[0] += 1

        ring_pages = pk(O_U + 40960, 8192 + 16384 + 16896)
        ring_keys = [('K', hf, m) for hf in range(2) for m in range(8)] + [('V', hf, kt, i) for hf in range(2) for kt in range(4) for i in range(2)] \
            + [('vones', 0), ('vones', 1)] + [k for c in range(8) for k in HGk(c)]
        P.op('dve', lambda e: e.memset(TSJ[:, 0:1], 0.0), W=ring_pages + ring_keys)
        for hf in range(2):
            P.op('dve', lambda e, hf=hf: e.memset(vones(hf), 1.0), W=[('vones', hf)])
        P.op('dve', lambda e: e.tensor_scalar(out=vones(1), in0=vones(1), scalar1=cc(C_HASPREV), scalar2=None, op0=ALU.mult),
             R=['cst', ('vones', 1)], W=[('vones', 1)])

        X0COL = NPRE * T

        def layer0(b):
            rbs4 = [load_slot(S_WIN + i) for i in range(4)]
            layer0_mixer(b, rbs4, RH1, CAR1, 'rh1', 'car1')

        load_x(0, X0COL)
        load_x(1, X0COL + T)
        layer0(0)
        mlp(0, 0, S_UP0, S_DN0)
        kv_phase(0, 1, True)
        att_barrier()
        P.op('dve', lambda e: e.tensor_scalar(out=CAR1[:], in0=CAR1[:], scalar1=cc(C_HASPREV), scalar2=None, op0=ALU.mult),
             R=[('car1', m) for m in range(8)] + ['cst'], W=[('car1', m) for m in range(8)])
        for t in range(NT):
            b = (t + 1) % 2
            half = t % 2
            if t + 1 < NT:
                load_x(t % 2, X0COL + T + (t + 1) * T)
            if t == 1:
                P.op('dve', lambda e: e.memset(vones(1), 1.0), W=[('vones', 1)])
            layer0(b)
            mlp(b, 0, S_UP0, S_DN0)
            kv_phase(b, half, False)
            attention(b, half)
            att_barrier()
            PRENORM[0] = False
            if t + 1 < NT:
                nb_ = t % 2
                norm_stats(nb_, 1)
                norm_apply(nb_, 1, 1, C_ANORM)
                PRENORM[0] = True
            mlp(b, 1, S_UP1, S_DN1, store_t0=t * T)
        outkeys = [('out', t * T) for t in range(NT)]
        P.op('pool', None, R=outkeys)
        P.emit(nc)
    return nc


def _bias_table(rel_bias):
    j = np.arange(640)[:, None]
    q = np.arange(128)[None, :]
    u = q // 64
    qi = q % 64
    kj = j - 64 * u
    valid = (kj >= 0) & (kj < 576)
    dist = qi + 512 - kj
    idx = np.clip(dist, -63, 128) + 63
    tab = rel_bias[:, idx]
    tab = np.where(valid[None], tab, np.float32(-30000.0)).astype(np.float32)
    tab = tab.reshape(NH, 5, 128, 128)
    return np.ascontiguousarray(tab.transpose(2, 0, 1, 3).reshape(128, NH, 640))


def _prep_inputs(x, a_norm, a_w_in, a_conv_w, a_conv_b, a_w_gate, a_b_gate, a_lambda, a_w_out, kv_norm, w_kv,
                 k_norm, b_norm, b_w_q, b_q_norm, b_rel_bias, b_w_o, mlp_norm, w_up, w_down):
    f32 = np.float32
    x = np.asarray(x, f32)
    chunked = lambda v: np.asarray(v, f32).reshape(8, 128).T
    cst = np.zeros((128, NCST), f32)
    cst[:, C_ANORM:C_ANORM + 8] = chunked(a_norm[0])
    for k in range(4):
        cst[:, C_CONVW + 8 * k:C_CONVW + 8 * k + 8] = chunked(a_conv_w[0, k])
    cst[:, C_CONVB:C_CONVB + 8] = chunked(a_conv_b[0])
    bg = np.asarray(a_b_gate[0], f32)
    cst[:, C_BGR:C_BGR + 8] = bg[:, :128].T
    cst[:, C_BGI:C_BGI + 8] = bg[:, 128:].T
    cst[:, C_LAM:C_LAM + 8] = chunked(a_lambda[0])
    cst[:, C_MLP0:C_MLP0 + 8] = chunked(mlp_norm[0])
    cst[:, C_MLP1:C_MLP1 + 8] = chunked(mlp_norm[1])
    cst[:, C_KVN:C_KVN + 8] = chunked(kv_norm)
    cst[:, C_BN:C_BN + 8] = chunked(b_norm[0])
    cst[:, C_GK] = np.tile(np.asarray(k_norm, f32), 2)
    cst[:, C_GQ] = np.tile(np.asarray(b_q_norm[0], f32), 2)
    mats = np.zeros((128, 3, 128), f32)
    mats[:, 0, :] = 1.0
    mats[:64, 1, :64] = 1.0
    mats[64:, 1, 64:] = 1.0
    mats[:, 2, :] = np.eye(128, dtype=f32)
    relb = _bias_table(np.asarray(b_rel_bias[0], f32))
    shared = {
        "w_in": np.ascontiguousarray(a_w_in[0], f32), "w_gate": np.ascontiguousarray(a_w_gate[0], f32),
        "w_out": np.ascontiguousarray(a_w_out[0], f32), "w_kv": np.ascontiguousarray(w_kv, f32),
        "w_q": np.ascontiguousarray(b_w_q[0], f32), "w_o": np.ascontiguousarray(b_w_o[0], f32),
        "w_up": np.ascontiguousarray(w_up, f32), "w_down": np.ascontiguousarray(w_down, f32),
        "relb": relb, "mats": mats,
    }
    in_maps = []
    for j in range(NCORES):
        bi, s = j // 4, j % 4
        n = (s + 1) * SEG
        xt = np.zeros((D, TOK), f32)
        xt[:, TOK - n:] = x[bi, :n, :].T
        c = cst.copy()
        c[:, C_HASPREV] = 1.0 if s > 0 else 0.0
        for p in range(NPRE):
            c[:, C_VALID + p] = 1.0 if p >= (NPRE + 1) - 8 * s else 0.0
        m = dict(shared)
        m["xT"] = xt
        m["cst"] = c
        in_maps.append(m)
    return in_maps


_NC_CACHE = {}


def kernel(**inputs):
    in_maps = _prep_inputs(**inputs)
    if "nc" not in _NC_CACHE:
        _NC_CACHE["nc"] = build_nc()
    res = run_bass_kernel_spmd(_NC_CACHE["nc"], in_maps, core_ids=list(range(NCORES)))
    out = np.zeros((2, 4 * SEG, D), np.float32)
    for j in range(NCORES):
        bi, s = j // 4, j % 4
        out[bi, s * SEG:(s + 1) * SEG, :] = res.results[j]["outT"].T
    return out
```
